# Optimizing a Trainium2 kernel written in Bass

```python
import jax, jax.numpy as jnp
from jax import lax
import numpy as np

D_MODEL = 2048
BATCH = 8
SEQ = 4096
DEPTH = 4
DEC_BATCH = 16
DEC_SEQ = 16
PAST_LEN = 1024

CHUNK = 64
N_EVEN = (DEPTH + 1) // 2
N_ODD = DEPTH // 2
HA = 8
DH = 128
DA = HA * DH
DB = D_MODEL - DA
KB = 31
DC = D_MODEL
KC = 3
D_FF = -(-8 * D_MODEL // (3 * 256)) * 256
Q_BLOCK = 128
D_IN_AB = 3 * DA + HA + 2 * DB
RMS_EPS = 1e-6
LN_EPS = 1e-5

kernel_name = "fox_conformer_shortconv_adaln_stream_step"


def rms_norm(x, g):
    x32 = x.astype(jnp.float32)
    y = x32 * lax.rsqrt(jnp.mean(x32 * x32, axis=-1, keepdims=True) + RMS_EPS)
    return (y * g.astype(jnp.float32)).astype(x.dtype)


def layer_norm(x, g, b):
    x32 = x.astype(jnp.float32)
    mu = jnp.mean(x32, axis=-1, keepdims=True)
    xc = x32 - mu
    var = jnp.mean(xc * xc, axis=-1, keepdims=True)
    return (xc * lax.rsqrt(var + LN_EPS) * g.astype(jnp.float32) + b.astype(jnp.float32)).astype(x.dtype)


def modulate(x, g, shift, scale):
    return rms_norm(x, g) * (1 + scale[:, None, :]) + shift[:, None, :]


def causal_dwconv(x, hist, w, bias=None):
    xp = jnp.concatenate([hist.astype(x.dtype), x], axis=1)
    y = lax.conv_general_dilated(xp, w[:, None, :].astype(x.dtype), window_strides=(1,), padding='VALID',
                                 dimension_numbers=('NWC', 'WIO', 'NWC'), feature_group_count=x.shape[-1])
    if bias is not None:
        y = y + bias
    new_hist = xp[:, -(w.shape[0] - 1):]
    return y, new_hist


def fox_prompt(q, k, v, logf):
    B, T, H, Dh = q.shape
    L = jnp.cumsum(logf.astype(jnp.float32), axis=1).transpose(0, 2, 1)
    nb = T // Q_BLOCK
    qb = q.reshape(B, nb, Q_BLOCK, H, Dh).transpose(1, 0, 2, 3, 4)
    Lq = L.reshape(B, H, nb, Q_BLOCK).transpose(2, 0, 1, 3)
    kpos = jnp.arange(T)
    scale = Dh ** -0.5

    def block(args):
        i, qi, Lqi = args
        s = jnp.einsum('bqhd,bkhd->bhqk', qi, k, preferred_element_type=jnp.float32) * scale
        s = s + Lqi[..., :, None] - L[..., None, :]
        qpos = i * Q_BLOCK + jnp.arange(Q_BLOCK)
        s = jnp.where(kpos[None, :] <= qpos[:, None], s, -jnp.inf)
        p = jax.nn.softmax(s, axis=-1)
        return jnp.einsum('bhqk,bkhd->bqhd', p.astype(v.dtype), v)

    out = lax.map(block, (jnp.arange(nb), qb, Lq))
    return out.transpose(1, 0, 2, 3, 4).reshape(B, T, H * Dh)


def fox_sample(q, k_all, v_all, logf_all, past):
    B, S, H, Dh = q.shape
    Tk = k_all.shape[1]
    L = jnp.cumsum(logf_all.astype(jnp.float32), axis=1).transpose(0, 2, 1)
    s = jnp.einsum('bqhd,bkhd->bhqk', q, k_all, preferred_element_type=jnp.float32) * (Dh ** -0.5)
    s = s + L[..., past:, None] - L[..., None, :]
    qpos = past + jnp.arange(S)
    s = jnp.where(jnp.arange(Tk)[None, :] <= qpos[:, None], s, -jnp.inf)
    p = jax.nn.softmax(s, axis=-1)
    out = jnp.einsum('bhqk,bkhd->bqhd', p.astype(v_all.dtype), v_all)
    return out.reshape(B, S, H * Dh)


def mixer_ab(h, kv_cache, convb_hist, w_in, b_f, dw_w, dw_b, ln_g, ln_b, w_out):
    B, T, _ = h.shape
    proj = h @ w_in
    q, k, v, fz, ga, gb = jnp.split(proj, [DA, 2 * DA, 3 * DA, 3 * DA + HA, 3 * DA + HA + DB], axis=-1)
    q = q.reshape(B, T, HA, DH)
    k = k.reshape(B, T, HA, DH)
    v = v.reshape(B, T, HA, DH)
    logf = jax.nn.log_sigmoid((fz + b_f).astype(jnp.float32))
    if kv_cache is None:
        att = fox_prompt(q, k, v, logf)
    else:
        ck, cv, cf = kv_cache
        att = fox_sample(q, jnp.concatenate([ck.astype(k.dtype), k], axis=1),
                         jnp.concatenate([cv.astype(v.dtype), v], axis=1),
                         jnp.concatenate([cf.astype(jnp.float32), logf], axis=1), ck.shape[1])
    u = ga * jax.nn.sigmoid(gb)
    uc, new_convb = causal_dwconv(u, convb_hist, dw_w, dw_b)
    z = jax.nn.silu(layer_norm(uc, ln_g, ln_b))
    y = jnp.concatenate([att.astype(h.dtype), z], axis=-1) @ w_out
    return y, k, v, logf, new_convb


def mixer_c(h, convc_hist, w_in, conv_w, w_out):
    bg, cg, xv = jnp.split(h @ w_in, 3, axis=-1)
    uc, new_hist = causal_dwconv(cg * xv, convc_hist, conv_w)
    return (bg * uc) @ w_out, new_hist


def swiglu(h, w_gate, w_up, w_down):
    return (jax.nn.silu(h @ w_gate) * (h @ w_up)) @ w_down


def trunk(x, c, p, cache):
    B = x.shape[0]
    ks, vs, fs, bs, cs = [], [], [], [], []
    for l in range(DEPTH):
        mod = jax.nn.silu(c) @ p['ada_w'][l] + p['ada_b'][l]
        sh1, sc1, g1, sh2, sc2, g2 = jnp.split(mod, 6, axis=-1)
        h = modulate(x, p['norm_mix_g'][l], sh1, sc1)
        e = l // 2
        if l % 2 == 0:
            if cache is None:
                kv = None
                bh = jnp.zeros((B, KB - 1, DB), x.dtype)
            else:
                kv = (cache['k'][e], cache['v'][e], cache['logf'][e])
                bh = cache['convb'][e]
            y, k, v, f, bnew = mixer_ab(h, kv, bh, p['w_in_ab'][e], p['b_f'][e], p['dw_b_w'][e],
                                        p['dw_b_bias'][e], p['ln_b_g'][e], p['ln_b_b'][e], p['w_out_ab'][e])
            ks.append(k); vs.append(v); fs.append(f); bs.append(bnew)
        else:
            chist = jnp.zeros((B, KC - 1, DC), x.dtype) if cache is None else cache['convc'][e]
            y, cnew = mixer_c(h, chist, p['w_in_c'][e], p['conv_c_w'][e], p['w_out_c'][e])
            cs.append(cnew)
        x = x + g1[:, None, :] * y
        h = modulate(x, p['norm_ffn_g'][l], sh2, sc2)
        x = x + g2[:, None, :] * swiglu(h, p['w_gate'][l], p['w_up'][l], p['w_down'][l])
    y_out = rms_norm(x, p['final_g'])
    return (y_out, jnp.stack(ks), jnp.stack(vs), jnp.stack(fs), jnp.stack(bs), jnp.stack(cs))


def setup_inputs(seed: int = 0) -> dict:
    key = jax.random.key(seed)
    ks = jax.random.split(key, 32)
    f32 = jnp.float32
    nrm = lambda k, shape, s=1.0: (jax.random.normal(k, shape, f32) * s)
    D = D_MODEL
    return {
        'x_prompt': nrm(ks[0], (BATCH, SEQ, D)),
        'x_sample': nrm(ks[1], (DEC_BATCH, DEC_SEQ, D)),
        'cache_k': nrm(ks[2], (N_EVEN, DEC_BATCH, PAST_LEN, HA, DH)),
        'cache_v': nrm(ks[3], (N_EVEN, DEC_BATCH, PAST_LEN, HA, DH)),
        'cache_logf': jax.nn.log_sigmoid(3.0 + nrm(ks[4], (N_EVEN, DEC_BATCH, PAST_LEN, HA))),
        'state_convb': nrm(ks[5], (N_EVEN, DEC_BATCH, KB - 1, DB), 0.5),
        'state_convc': nrm(ks[6], (N_ODD, DEC_BATCH, KC - 1, DC), 0.5),
        'c_prompt': nrm(ks[7], (BATCH, D)),
        'c_sample': nrm(ks[8], (DEC_BATCH, D)),
        'ada_w': nrm(ks[9], (DEPTH, D, 6 * D), 0.5 * D ** -0.5),
        'ada_b': nrm(ks[10], (DEPTH, 6 * D), 0.02),
        'norm_mix_g': 1.0 + nrm(ks[11], (DEPTH, D), 0.02),
        'norm_ffn_g': 1.0 + nrm(ks[12], (DEPTH, D), 0.02),
        'w_in_ab': nrm(ks[13], (N_EVEN, D, D_IN_AB), D ** -0.5),
        'b_f': 3.0 + nrm(ks[14], (N_EVEN, HA), 0.5),
        'dw_b_w': nrm(ks[15], (N_EVEN, KB, DB), KB ** -0.5),
        'dw_b_bias': nrm(ks[16], (N_EVEN, DB), 0.02),
        'ln_b_g': 1.0 + nrm(ks[17], (N_EVEN, DB), 0.02),
        'ln_b_b': nrm(ks[18], (N_EVEN, DB), 0.02),
        'w_out_ab': nrm(ks[19], (N_EVEN, DA + DB, D), (DA + DB) ** -0.5),
        'w_in_c': nrm(ks[20], (N_ODD, D, 3 * DC), D ** -0.5),
        'conv_c_w': nrm(ks[21], (N_ODD, KC, DC), KC ** -0.5),
        'w_out_c': nrm(ks[22], (N_ODD, DC, D), DC ** -0.5),
        'w_gate': nrm(ks[23], (DEPTH, D, D_FF), D ** -0.5),
        'w_up': nrm(ks[24], (DEPTH, D, D_FF), D ** -0.5),
        'w_down': nrm(ks[25], (DEPTH, D_FF, D), D_FF ** -0.5),
        'final_g': 1.0 + nrm(ks[26], (D,), 0.02),
    }


def reference(x_prompt, x_sample, cache_k, cache_v, cache_logf, state_convb, state_convc, c_prompt, c_sample,
              ada_w, ada_b, norm_mix_g, norm_ffn_g, w_in_ab, b_f, dw_b_w, dw_b_bias, ln_b_g, ln_b_b, w_out_ab,
              w_in_c, conv_c_w, w_out_c, w_gate, w_up, w_down, final_g):
    p = {'ada_w': ada_w, 'ada_b': ada_b, 'norm_mix_g': norm_mix_g, 'norm_ffn_g': norm_ffn_g,
         'w_in_ab': w_in_ab, 'b_f': b_f, 'dw_b_w': dw_b_w, 'dw_b_bias': dw_b_bias, 'ln_b_g': ln_b_g,
         'ln_b_b': ln_b_b, 'w_out_ab': w_out_ab, 'w_in_c': w_in_c, 'conv_c_w': conv_c_w, 'w_out_c': w_out_c,
         'w_gate': w_gate, 'w_up': w_up, 'w_down': w_down, 'final_g': final_g}
    cache = {'k': cache_k, 'v': cache_v, 'logf': cache_logf, 'convb': state_convb, 'convc': state_convc}
    y_prompt, k_p, v_p, logf_p, convb_p, convc_p = trunk(x_prompt, c_prompt, p, None)
    y_sample, k_s, v_s, logf_s, convb_s, convc_s = trunk(x_sample, c_sample, p, cache)
    return (y_prompt, y_sample, k_p, v_p, logf_p, convb_p, convc_p, k_s, v_s, logf_s, convb_s, convc_s)
```

```python
from contextlib import ExitStack
import os
import numpy as np
import concourse.bass as bass
import concourse.mybir as mybir
from concourse.bass_utils import run_bass_kernel_spmd

F32 = mybir.dt.float32
BF16 = mybir.dt.bfloat16
AF = mybir.ActivationFunctionType
ALU = mybir.AluOpType

D = 2048
NCH = 16
DFF = 5632
NFF = 44
HA = 8
DH = 128
DA = 1024
DB = 1024
KB = 31
KC = 3
DIN_AB = 5128
PAST = 1024
SS = 16
TT = 512
NEG = -30000.0
SCALE = DH ** -0.5
ENG = ("pe", "act", "dve", "pool", "sp")
NSLOT = 3
QSZ = (12, 10, 12, 10)
WCOLS = 520


class Buf:
    __slots__ = ("n", "w", "rs")

    def __init__(self, n):
        self.n = n
        self.w = None
        self.rs = {}


class Ctx:
    def __init__(self, nc, stack):
        self.nc = nc
        self.stack = stack
        self.ops = {e: [] for e in ENG}
        self.waited = {e: {} for e in ENG}
        self.semh = {}
        self.cnt = {}
        self.prog = {}
        self.passno = -1
        self.out_events = []
        self.dead = False
        self.kstop = os.environ.get("KSTOP", "")
        self.new_pass()

    def stage(self, name):
        if self.kstop and (name == self.kstop or ("p%d:%s" % (self.passno, name)) == self.kstop):
            self.dead = True

    def sem(self, key):
        if key not in self.semh:
            self.semh[key] = self.stack.enter_context(self.nc.semaphore("s%d" % len(self.semh)))
            self.cnt[key] = 0
        return key

    def new_pass(self):
        self.passno += 1
        for e in ("pe", "act", "dve", "pool"):
            self.prog[e] = self.sem(("prog", e, self.passno))

    def _collect(self, reads, writes):
        evs = []
        for b in reads:
            if b.w is not None:
                evs.append(b.w)
        for b in writes:
            if b.rs:
                evs.extend(b.rs.items())
            elif b.w is not None:
                evs.append(b.w)
        return evs

    def _waits(self, eng, evs):
        need = {}
        for (k, v) in evs:
            if eng == "pe" and k[0] == "prog" and k[1] == "pe":
                continue
            if need.get(k, 0) < v:
                need[k] = v
        wd = self.waited[eng]
        for k, v in need.items():
            if wd.get(k, 0) >= v:
                continue
            wd[k] = v
            h = self.semh[k]
            self.ops[eng].append(lambda E, h=h, v=v: E.wait_ge(h, v))

    def _update(self, ev, reads, writes):
        k, v = ev
        for b in reads:
            if b.rs.get(k, 0) < v:
                b.rs[k] = v
        for b in writes:
            b.w = ev
            b.rs = {}

    def op(self, eng, fn, reads=(), writes=()):
        if self.dead:
            return None
        self._waits(eng, self._collect(reads, writes))
        k = self.prog[eng]
        self.cnt[k] += 1
        ev = (k, self.cnt[k])
        h = self.semh[k]
        self.ops[eng].append(lambda E, fn=fn, h=h: fn(E).then_inc(h, 1))
        self._update(ev, reads, writes)
        return ev

    def dma(self, q, pairs, semkey, reads=(), writes=(), is_out=False):
        if self.dead:
            return None
        self._waits(q, self._collect(reads, writes))
        k = self.sem(semkey)
        h = self.semh[k]
        for (o, i) in pairs:
            self.cnt[k] += 16
            self.ops[q].append(lambda E, o=o, i=i, h=h: E.dma_start(out=o, in_=i).then_inc(h, 16))
        ev = (k, self.cnt[k])
        self._update(ev, reads, writes)
        if is_out:
            self.out_events.append(ev)
        return ev

    def alias(self, old, new):
        evs = {}
        for b in old:
            if b.w is not None and evs.get(b.w[0], 0) < b.w[1]:
                evs[b.w[0]] = b.w[1]
            for k, v in b.rs.items():
                if evs.get(k, 0) < v:
                    evs[k] = v
        for b in new:
            b.w = None
            b.rs = dict(evs)


class PAlloc:
    def __init__(self):
        self.off = {}
        self.n = 0

    def add(self, name, w):
        self.off[name] = (self.n, w)
        self.n += w


def param_layout():
    P = PAlloc()
    for l in range(4):
        P.add(("adab", l), 96)
        P.add(("gmix", l), 16)
        P.add(("gffn", l), 16)
    P.add("gfin", 16)
    for e in range(2):
        P.add(("bf", e), 8)
        P.add(("dww", e), 8 * 31)
        P.add(("dwb", e), 8)
        P.add(("lng", e), 8)
        P.add(("lnb", e), 8)
        P.add(("ccw", e), 16 * 3)
        P.add(("histb", e), 2 * 8 * 30)
        P.add(("histc", e), 2 * 16 * 2)
    P.add("cT", 16 * 3)
    return P


CONST_W = 128 * 3 + 384


def make_consts():
    c = np.zeros((128, CONST_W), np.float32)
    c[:, 0:128] = np.eye(128, dtype=np.float32)
    s = np.arange(128)
    c[:, 128:256] = (s[:, None] <= s[None, :]).astype(np.float32)
    c[:, 256:384] = np.where(s[:, None] <= s[None, :], 0.0, NEG)
    for k in range(8):
        g = k // 3
        c[k, 384 + g * 128 + 32 * (k % 3)] = 1.0
    return c


def build(T, with_sample=True, dbg=None):
    NT = T // TT
    nc = bass.Bass("TRN2", target_bir_lowering=False)
    P = param_layout()
    dt_in = lambda name, shape: nc.dram_tensor(name, list(shape), F32, kind="ExternalInput").ap()
    dt_out = lambda name, shape: nc.dram_tensor(name, list(shape), F32, kind="ExternalOutput").ap()
    xT = dt_in("xT", (D, T))
    xsT = dt_in("xsT", (D, 32))
    ck = dt_in("ck", (2, 2, PAST, DA))
    cv = dt_in("cv", (2, 2, PAST, DA))
    cf = dt_in("cf", (2, 2, PAST, HA))
    prm_d = dt_in("prm", (128, P.n))
    cst_d = dt_in("cst", (128, CONST_W))
    ada_w = dt_in("ada_w", (4, D, 6 * D))
    w_in_ab = dt_in("w_in_ab", (2, D, DIN_AB))
    w_out_ab = dt_in("w_out_ab", (2, D, D))
    w_in_c = dt_in("w_in_c", (2, D, 3 * D))
    w_out_c = dt_in("w_out_c", (2, D, D))
    w_gate = dt_in("w_gate", (4, D, DFF))
    w_up = dt_in("w_up", (4, D, DFF))
    w_down = dt_in("w_down", (4, DFF, D))
    y_o = dt_out("y", (T, D))
    ys_o = dt_out("ys", (32, D))
    kp_o = dt_out("k_p", (2, T, DA))
    vp_o = dt_out("v_p", (2, T, DA))
    lfp_o = dt_out("logf_p", (2, T, HA))
    cbp_o = dt_out("convb_p", (2, 30, DB))
    ccp_o = dt_out("convc_p", (2, 2, D))
    ks_o = dt_out("k_s", (2, 2, SS, DA))
    vs_o = dt_out("v_s", (2, 2, SS, DA))
    lfs_o = dt_out("logf_s", (2, 2, SS, HA))
    cbs_o = dt_out("convb_s", (2, 2, 30, DB))
    ccs_o = dt_out("convc_s", (2, 2, 2, D))
    NBLK = T // 128
    KTs = nc.dram_tensor("KTs", [2, HA, 128, T], BF16, kind="Internal").ap()
    Vs = nc.dram_tensor("Vs", [2, HA, 128, NBLK, 128], BF16, kind="Internal").ap()
    dbg_o = {}
    if dbg:
        for name, shape in dbg.items():
            dbg_o[name] = dt_out("dbg_" + name, shape)

    stack = ExitStack()
    with stack:
        sb = lambda name, shape, dt=F32: stack.enter_context(nc.sbuf_tensor(name, list(shape), dt))
        ps = lambda name, shape, dt=F32: stack.enter_context(nc.psum_tensor(name, list(shape), dt))
        cx = Ctx(nc, stack)

        PRM = sb("PRM", (128, P.n))
        CSTF = sb("CSTF", (128, CONST_W))
        IDb = sb("IDb", (128, 128), BF16)
        NMb = sb("NMb", (128, 128), BF16)
        ONEb = sb("ONEb", (128, 128), BF16)
        ONEn = sb("ONEn", (128, 128), BF16)
        ONEm = sb("ONEm", (128, 128), BF16)
        ONEf = sb("ONEf", (128, 128))
        CC = sb("CC", (128, 4))
        MOD = sb("MOD", (128, 4 * 6 * 16 * 3))
        AA = sb("AA", (128, 4 * 2 * 16 * 3))
        scT = sb("scT", (128, 16 * 3), BF16)
        X = sb("X", (128, NCH, TT))
        H = sb("H", (128, NCH, TT), BF16)
        WR = [sb("WR%d" % i, (128, 16, WCOLS), BF16) for i in range(NSLOT)]
        QT = sb("QT", (128, HA, TT), BF16)
        KVR = sb("KVR", (128, 4096), BF16)
        KTH = [KVR[:, i * 512:(i + 1) * 512] for i in range(4)]
        VH = [KVR[:, 2048 + i * 512:2048 + (i + 1) * 512].rearrange("p (a d) -> p a d", d=128) for i in range(4)]
        CK = [KVR[:, 0:1024].rearrange("p (a d) -> p a d", d=128)] * 2
        CVv = [KVR[:, 1024:2048].rearrange("p (a d) -> p a d", d=128)] * 2
        KTC = [KVR[:, 2048:3072]] * 2
        R1t = sb("R1t", (128, 6144))
        MOt = sb("MOt", (128, 16, TT), BF16)
        RSTD = sb("RSTD", (128, TT))
        TMPF = [sb("TMPF%d" % i, (128, TT)) for i in range(3)]
        SQ = [sb("SQ%d" % i, (128, TT), BF16) for i in range(2)]
        UCB = [sb("UCB%d" % i, (128, TT), BF16) for i in range(2)]
        PT = [sb("PT%d" % i, (128, TT), BF16) for i in range(3)]
        KVS = [sb("KVS%d" % i, (128, TT)) for i in range(2)]
        LF = sb("LF", (128, 4, 8))
        LFC = sb("LFC", (128, 2, 8, 8))
        DC = sb("DC", (8, 8))
        CCOL = sb("CCOL", (8, 2))
        LQ = sb("LQ", (128, 3, TT), BF16)
        LQS = sb("LQS", (1, 256), BF16)
        NEGL = sb("NEGL", (128, 2, max(NBLK, 1), 8))
        NEGLS = sb("NEGLS", (128, 2, 9, 8))
        HALOB = sb("HALOB", (128, 2, 8, 30))
        HALOC = sb("HALOC", (128, 2, 16, 2))
        MEANB = sb("MEANB", (128, TT))
        LT = MEANB[0:8, :]
        def Uv(i):
            return R1t[:, i * 544:i * 544 + 542]

        def UCv(c):
            return R1t[:, 1632 + c * TT:1632 + (c + 1) * TT]

        def YSv(i):
            return R1t[:, i * 2048:(i + 1) * 2048]

        KBF = R1t[:, 0:2048].bitcast(BF16)
        VBF = R1t[:, 2048:4096].bitcast(BF16)
        KTT = R1t[:, 4096:6144].bitcast(BF16)

        PB = [ps("PB%d" % i, (128, TT)) for i in range(7)]
        PTP = ps("PTP", (128, 1024), BF16)

        bX = [Buf("x%d" % c) for c in range(NCH)]
        bH = [Buf("h%d" % c) for c in range(NCH)]
        bWR = [Buf("wr%d" % i) for i in range(NSLOT)]
        bQT = [Buf("qt%d" % h) for h in range(HA)]
        bKTH = [Buf("kth%d" % i) for i in range(4)]
        bVH = [Buf("vh%d" % i) for i in range(4)]
        bU = [Buf("u%d" % i) for i in range(3)]
        bUC = [Buf("uc%d" % c) for c in range(8)]
        bYS = [Buf("ys%d" % i) for i in range(2)]
        bKBF = Buf("kbf")
        bVBF = Buf("vbf")
        bKTT = Buf("ktt")
        bMO = [Buf("mo%d" % c) for c in range(16)]
        bRSTD = Buf("rstd")
        bTMPF = [Buf("tmpf%d" % i) for i in range(3)]
        bSQ = [Buf("sq%d" % i) for i in range(2)]
        bUCB = [Buf("ucb%d" % i) for i in range(2)]
        bPT = [Buf("pt%d" % i) for i in range(3)]
        bKVS = [Buf("kvs%d" % i) for i in range(2)]
        bLF = Buf("lf")
        bLFC = Buf("lfc")
        bDC = Buf("dc")
        bCCOL = [Buf("ccol0"), Buf("ccol1")]
        bLQ = Buf("lq")
        bNEGL = [Buf("negl0"), Buf("negl1")]
        bNEGLS = Buf("negls")
        bHALOB = [[Buf("hb%d_%d" % (e, c)) for c in range(8)] for e in range(2)]
        bHALOC = [[Buf("hc%d_%d" % (e, c)) for c in range(16)] for e in range(2)]
        bMEANB = Buf("meanb")
        bLT = bMEANB
        bPB = [Buf("pb%d" % i) for i in range(7)]
        bPTP = Buf("ptp")
        bCK = [Buf("ck0")] * 2
        bCV = [Buf("cv0")] * 2
        bKTC = [Buf("ktc0")] * 2
        bKTs = {}
        bVs = {}
        bMOD = Buf("mod")
        rr = {"tmpf": 0, "sq": 0, "ucb": 0, "pt": 0, "kvs": 0, "main": 0, "u": 0, "kth": 0, "st": 0, "ys": 0}

        def ring(name, n):
            i = rr[name] % n
            rr[name] += 1
            return i

        def pp(name, j=0, w=1):
            o, _ = P.off[name]
            return PRM[:, o + j:o + j + w]

        ev_prm = cx.dma("sp", [(PRM[:], prm_d)], ("ld", "prm"))
        ev_cst = cx.dma("sp", [(CSTF[:], cst_d)], ("ld", "cst"))
        bCST = Buf("cst")
        bCST.w = ev_cst
        bPRM = Buf("prm")
        bPRM.w = ev_prm
        bC2 = Buf("c2")
        cx.op("dve", lambda E: E.tensor_copy(out=IDb[:], in_=CSTF[:, 0:128]), reads=[bCST], writes=[bC2])
        cx.op("dve", lambda E: E.tensor_copy(out=NMb[:], in_=CSTF[:, 256:384]), reads=[bCST], writes=[bC2])
        cx.op("dve", lambda E: E.memset(ONEb[:], 1.0), writes=[bC2])
        cx.op("dve", lambda E: E.memset(ONEn[:], 1.0 / D), writes=[bC2])
        cx.op("dve", lambda E: E.memset(ONEm[:], 1.0 / DB), writes=[bC2])
        cx.op("dve", lambda E: E.memset(ONEf[:], 1.0), writes=[bC2])
        cx.op("dve", lambda E: E.memset(CC[:, 0:1], 0.0), writes=[bC2])
        cx.op("dve", lambda E: E.memset(CC[:, 1:2], 1.0), writes=[bC2])
        cx.op("dve", lambda E: E.memset(CC[:, 2:3], 1e-6), writes=[bC2])
        cx.op("dve", lambda E: E.memset(CC[:, 3:4], 1e-5), writes=[bC2])
        cx.op("dve", lambda E: E.memset(CCOL[:], 0.0), writes=bCCOL)
        cx.op("dve", lambda E: E.memset(HALOB[:], 0.0), writes=[b for r in bHALOB for b in r])
        evc = cx.op("dve", lambda E: E.memset(HALOC[:], 0.0), writes=[b for r in bHALOC for b in r])
        for e in ("pe", "act", "pool"):
            cx._waits(e, [evc, ev_prm, ev_cst])
        cx._waits("dve", [ev_prm, ev_cst])
        IDf = CSTF[:, 0:128]
        TRI = CSTF[:, 128:256]
        SEL = lambda g: CSTF[0:8, 384 + g * 128:384 + (g + 1) * 128]
        ZERO = CC[:, 0:1]
        ONE = CC[:, 1:2]
        EPS_R = CC[:, 2:3]
        EPS_L = CC[:, 3:4]

        cx.stage("consts")
        def wsrc(Wap, r0, nk, c0, n):
            return Wap[r0 * 128:(r0 + nk) * 128, c0:c0 + n].rearrange("(kc p) n -> p kc n", p=128)

        def layer_blocks(l):
            e = l // 2
            bl = []
            if l % 2 == 0:
                W = w_in_ab[e]
                bl.append([(16, 0, 512, wsrc(W, 0, 16, 1024, 512))])
                bl.append([(16, 0, 512, wsrc(W, 0, 16, 1536, 512))])
                bl.append([(16, 0, 512, wsrc(W, 0, 16, 2048, 512))])
                bl.append([(16, 0, 512, wsrc(W, 0, 16, 2560, 512)),
                           (16, 512, 8, wsrc(W, 0, 16, 3072, 8))])
                bl.append([(16, 0, 512, wsrc(W, 0, 16, 0, 512))])
                bl.append([(16, 0, 512, wsrc(W, 0, 16, 512, 512))])
                for j in range(4):
                    bl.append([(16, 0, 256, wsrc(W, 0, 16, 3080 + j * 256, 256)),
                               (16, 256, 256, wsrc(W, 0, 16, 4104 + j * 256, 256))])
                Wo = w_out_ab[e]
            else:
                W = w_in_c[e]
                for j in range(16):
                    bl.append([(16, 0, 128, wsrc(W, 0, 16, j * 128, 128)),
                               (16, 128, 128, wsrc(W, 0, 16, 2048 + j * 128, 128)),
                               (16, 256, 128, wsrc(W, 0, 16, 4096 + j * 128, 128))])
                Wo = w_out_c[e]
            for cb in range(4):
                bl.append([(16, 0, 512, wsrc(Wo, 0, 16, cb * 512, 512))])
            qoff = 0
            for qs in QSZ:
                for jb in range(qs // 2):
                    c0 = qoff + 2 * jb
                    bl.append([(16, 0, 256, wsrc(w_gate[l], 0, 16, c0 * 128, 256)),
                               (16, 256, 256, wsrc(w_up[l], 0, 16, c0 * 128, 256))])
                for cb in range(4):
                    bl.append([(qs, 0, 512, wsrc(w_down[l], qoff, qs, cb * 512, 512))])
                qoff += qs
            return bl

        n_pass = NT + (1 if with_sample else 0)
        blocks = []
        for l in range(4):
            for j in range(24):
                blocks.append([(16, 0, 512, wsrc(ada_w[l], 0, 16, j * 512, 512))])
        per_pass = [b for l in range(4) for b in layer_blocks(l)]
        for _ in range(n_pass):
            blocks.extend(per_pass)
        ws = {"load": 0, "use": 0}
        NADA = 96
        NPB = len(per_pass)
        WCH = 48
        Wscr_l = [nc.dram_tensor("Wscr%d" % k, [WCH, 128, 16, WCOLS], BF16, kind="Internal").ap()
                  for k in range((NPB + WCH - 1) // WCH)]

        class _W:
            def __getitem__(self, key):
                pj = key[0]
                return Wscr_l[pj // WCH][(pj % WCH,) + tuple(key[1:])]
        Wscr = _W()
        bWscr = {}

        def blk_ext(j):
            nk = max(p[0] for p in blocks[j])
            ncols = max(p[1] + p[2] for p in blocks[j])
            return nk, ncols

        def wget():
            i = ws["use"]
            ws["use"] += 1
            if cx.dead:
                return WR[i % NSLOT], bWR[i % NSLOT]
            jp = i - 1
            if n_pass > 1 and jp >= NADA and (jp - NADA) < NPB:
                pj = jp - NADA
                sp_ = jp % NSLOT
                nk, ncols = blk_ext(jp)
                bWscr[pj] = Buf("wscr%d" % pj)
                cx.dma("sp", [(Wscr[pj, :, 0:nk, 0:ncols], WR[sp_][:, 0:nk, 0:ncols])], ("wst", sp_),
                       reads=[bWR[sp_]], writes=[bWscr[pj]])
            while ws["load"] < min(len(blocks), i + NSLOT):
                j = ws["load"]
                s = j % NSLOT
                if j >= NADA + NPB:
                    pj = (j - NADA) % NPB
                    nk, ncols = blk_ext(j)
                    cx.dma("pool", [(WR[s][:, 0:nk, 0:ncols], Wscr[pj, :, 0:nk, 0:ncols])], ("w", s),
                           reads=[bWscr[pj]], writes=[bWR[s]])
                else:
                    pairs = [(WR[s][:, 0:nk, c0:c0 + n], src) for (nk, c0, n, src) in blocks[j]]
                    cx.dma("pool", pairs, ("w", s), writes=[bWR[s]])
                ws["load"] += 1
            return WR[i % NSLOT], bWR[i % NSLOT]

        def mm(bank, out_ap, pairs, reads, start=True, stop=True):
            def fn(E, out_ap=out_ap, pairs=pairs, start=start, stop=stop):
                n = len(pairs)
                ins = None
                for i, (l_, r_) in enumerate(pairs):
                    ins = E.matmul(out_ap, l_, r_, start=(start and i == 0), stop=(stop and i == n - 1))
                return ins
            return cx.op("pe", fn, reads=reads, writes=[bank])

        cTo = P.off["cT"][0]
        cx.op("act", lambda E: E.activation(out=scT[:], in_=PRM[:, cTo:cTo + 48], func=AF.Silu), writes=[bMOD])
        bSCT = Buf("sct")
        bSCT.w = bMOD.w
        for l in range(4):
            for j in range(24):
                Wt, bw = wget()
                kind, cg = j // 4, (j % 4) * 4
                for c4 in range(4):
                    b = ring("main", 4)
                    mm(bPB[b], PB[b][:, 0:3],
                       [(Wt[:, kc, c4 * 128:(c4 + 1) * 128], scT[:, kc * 3:kc * 3 + 3]) for kc in range(16)],
                       reads=[bw, bSCT])
                    mo = ((l * 6 + kind) * 16 + cg + c4) * 3
                    cx.op("dve", lambda E, b=b, mo=mo, l=l, jj=kind * 16 + cg + c4: E.tensor_scalar(
                        out=MOD[:, mo:mo + 3], in0=PB[b][:, 0:3], scalar1=pp(("adab", l), jj), scalar2=None,
                        op0=ALU.add), reads=[bPB[b]], writes=[bMOD])

        cx.stage("mods")

        def modv(l, kind, c, seq):
            o = ((l * 6 + kind) * 16 + c) * 3 + seq
            return MOD[:, o:o + 1]

        def aav(l, which, c, seq):
            o = ((l * 2 + which) * 16 + c) * 3 + seq
            return AA[:, o:o + 1]

        for l in range(4):
            for which, (gname, kind) in enumerate(((("gmix", l), 1), (("gffn", l), 4))):
                for c in range(16):
                    mo = ((l * 6 + kind) * 16 + c) * 3
                    ao = ((l * 2 + which) * 16 + c) * 3
                    cx.op("dve", lambda E, mo=mo, ao=ao, gname=gname, c=c: E.tensor_scalar(
                        out=AA[:, ao:ao + 3], in0=MOD[:, mo:mo + 3], scalar1=1.0, scalar2=pp(gname, c),
                        op0=ALU.add, op1=ALU.mult), reads=[bMOD], writes=[bMOD])
        if not cx.dead:
            for e in ("pe", "act", "pool"):
                cx._waits(e, [bMOD.w])

        def norm_to_h(tile, Acol, Bcol, to_f32=None):
            n = tile["n"]
            for c in range(NCH):
                i = ring("sq", 2)
                cx.op("act", lambda E, c=c, i=i: E.activation(out=SQ[i][:, 0:n], in_=X[:, c, 0:n], func=AF.Square),
                      reads=[bX[c]], writes=[bSQ[i]])
                mm(bPB[4], PB[4][:, 0:n], [(ONEn[:], SQ[i][:, 0:n])], reads=[bSQ[i]], start=(c == 0), stop=(c == NCH - 1))
            cx.op("act", lambda E: E.activation(out=RSTD[:, 0:n], in_=PB[4][:, 0:n], func=AF.Sqrt, bias=EPS_R, scale=1.0),
                  reads=[bPB[4]], writes=[bRSTD])
            cx.op("dve", lambda E: E.reciprocal(out=RSTD[:, 0:n], in_=RSTD[:, 0:n]), reads=[bRSTD], writes=[bRSTD])
            for c in range(NCH):
                for (c0, w, seq) in tile["segs"]:
                    if to_f32 is not None:
                        dst, bd = to_f32(c)
                        cx.op("dve", lambda E, c=c, c0=c0, w=w, seq=seq, dst=dst: E.scalar_tensor_tensor(
                            out=dst[:, c0:c0 + w], in0=X[:, c, c0:c0 + w], scalar=Acol(c, seq), in1=RSTD[:, c0:c0 + w],
                            op0=ALU.mult, op1=ALU.mult), reads=[bX[c], bRSTD], writes=[bd])
                    else:
                        i = ring("tmpf", 3)
                        cx.op("dve", lambda E, c=c, c0=c0, w=w, seq=seq, i=i: E.scalar_tensor_tensor(
                            out=TMPF[i][:, c0:c0 + w], in0=X[:, c, c0:c0 + w], scalar=Acol(c, seq), in1=RSTD[:, c0:c0 + w],
                            op0=ALU.mult, op1=ALU.mult), reads=[bX[c], bRSTD], writes=[bTMPF[i]])
                        cx.op("act", lambda E, c=c, c0=c0, w=w, seq=seq, i=i: E.activation(
                            out=H[:, c, c0:c0 + w], in_=TMPF[i][:, c0:c0 + w], func=AF.Identity, bias=Bcol(c, seq), scale=1.0),
                            reads=[bTMPF[i]], writes=[bH[c]])

        def resid_add(tile, oc, b, l, kind):
            for (c0, w, seq) in tile["segs"]:
                cx.op("dve", lambda E, c0=c0, w=w, seq=seq: E.scalar_tensor_tensor(
                    out=X[:, oc, c0:c0 + w], in0=PB[b][:, c0:c0 + w], scalar=modv(l, kind, oc, seq), in1=X[:, oc, c0:c0 + w],
                    op0=ALU.mult, op1=ALU.add), reads=[bPB[b], bX[oc]], writes=[bX[oc]])

        def out_proj(tile, l):
            n = tile["n"]
            for cb in range(4):
                Wt, bw = wget()
                for c4 in range(4):
                    oc = cb * 4 + c4
                    b = ring("main", 4)
                    mm(bPB[b], PB[b][:, 0:n],
                       [(Wt[:, kc, c4 * 128:(c4 + 1) * 128], MOt[:, kc, 0:n]) for kc in range(16)],
                       reads=[bw] + bMO[0:16])
                    resid_add(tile, oc, b, l, 2)

        def ffn(tile, l):
            n = tile["n"]
            norm_to_h(tile, lambda c, s: aav(l, 1, c, s), lambda c, s: modv(l, 3, c, s))
            for qs in QSZ:
                for jb in range(qs // 2):
                    Wt, bw = wget()
                    for cc in range(2):
                        hc = 2 * jb + cc
                        bg = ring("main", 4)
                        mm(bPB[bg], PB[bg][:, 0:n],
                           [(Wt[:, kc, cc * 128:(cc + 1) * 128], H[:, kc, 0:n]) for kc in range(16)], reads=[bw] + bH)
                        bu = ring("main", 4)
                        mm(bPB[bu], PB[bu][:, 0:n],
                           [(Wt[:, kc, 256 + cc * 128:256 + (cc + 1) * 128], H[:, kc, 0:n]) for kc in range(16)], reads=[bw] + bH)
                        i = ring("tmpf", 3)
                        cx.op("act", lambda E, bg=bg, i=i: E.activation(out=TMPF[i][:, 0:n], in_=PB[bg][:, 0:n], func=AF.Silu),
                              reads=[bPB[bg]], writes=[bTMPF[i]])
                        cx.op("dve", lambda E, bu=bu, i=i, hc=hc: E.tensor_tensor(
                            out=MOt[:, hc, 0:n], in0=PB[bu][:, 0:n], in1=TMPF[i][:, 0:n], op=ALU.mult),
                            reads=[bPB[bu], bTMPF[i]], writes=[bMO[hc]])
                for cb in range(4):
                    Wt, bw = wget()
                    for c4 in range(4):
                        b = ring("main", 4)
                        mm(bPB[b], PB[b][:, 0:n],
                           [(Wt[:, kk, c4 * 128:(c4 + 1) * 128], MOt[:, kk, 0:n]) for kk in range(qs)],
                           reads=[bw] + bMO[0:qs])
                        resid_add(tile, cb * 4 + c4, b, l, 5)

        def logf_and_L(tile, e, fzbanks):
            kind = tile["kind"]
            tbl = tile["tblocks"]
            for (pap, bb, m, tb) in fzbanks:
                cx.op("dve", lambda E, pap=pap, m=m, tb=tb: E.tensor_tensor(
                    out=LF[0:m, tb, :], in0=pap, in1=pp(("bf", e), 0, 8)[0:m, :], op=ALU.add), reads=[bb], writes=[bLF])
            nb = len(tbl)
            m0 = tbl[0][1]
            cx.op("act", lambda E: E.activation(out=LF[0:m0, 0:nb, :], in_=LF[0:m0, 0:nb, :], func=AF.Exp, scale=-1.0),
                  reads=[bLF], writes=[bLF])
            cx.op("act", lambda E: E.activation(out=LF[0:m0, 0:nb, :], in_=LF[0:m0, 0:nb, :], func=AF.Ln, bias=ONE[0:m0, :], scale=1.0),
                  reads=[bLF], writes=[bLF])
            cx.op("dve", lambda E: E.tensor_scalar(out=LF[0:m0, 0:nb, :], in0=LF[0:m0, 0:nb, :], scalar1=-1.0, scalar2=None, op0=ALU.mult),
                  reads=[bLF], writes=[bLF])
            if kind == "prompt":
                t0 = tile["t0"]
                ev = cx.dma("sp", [(lfp_o[e, t0:t0 + TT, :].rearrange("(tb p) h -> p tb h", p=128), LF[:, :, :])],
                            ("st", "lf"), reads=[bLF], is_out=True)
                cx.op("dve", lambda E: E.tensor_scalar(out=DC[:], in0=IDf[0:8, 0:8], scalar1=CCOL[:, e:e + 1], scalar2=None,
                                                       op0=ALU.mult), reads=[bCCOL[e]], writes=[bDC])
                g0 = t0 // 128
                for tb in range(4):
                    prs = [(LF[:, t2, :], ONEf[:]) for t2 in range(tb)] + [(LF[:, tb, :], TRI)]
                    mm(bPB[6], PB[6][0:8, tb * 128:(tb + 1) * 128], prs, reads=[bLF], start=True, stop=True)
                    prs = [(ONEf[:], LF[:, t2, :]) for t2 in range(tb)] + [(TRI, LF[:, tb, :]), (ONEf[0:8, :], DC[:])]
                    mm(bPB[5], PB[5][:, tb * 8:(tb + 1) * 8], prs, reads=[bLF, bDC])
                cx.op("dve", lambda E: E.tensor_scalar(out=LT[:, :], in0=PB[6][0:8, :], scalar1=CCOL[:, e:e + 1], scalar2=None,
                                                       op0=ALU.add), reads=[bPB[6], bCCOL[e]], writes=[bLT])
                cx.op("dve", lambda E: E.tensor_scalar(out=NEGL[:, e, g0:g0 + 4, :], in0=PB[5][:, 0:32].rearrange("p (a b) -> p a b", b=8),
                                                       scalar1=-1.0, scalar2=None, op0=ALU.mult), reads=[bPB[5]], writes=[bNEGL[e]])
                cx.op("dve", lambda E: E.tensor_copy(out=CCOL[:, e:e + 1], in_=LT[:, TT - 1:TT]), reads=[bLT], writes=[bCCOL[e]])
                ncol = TT
            else:
                for s in range(2):
                    cx.dma("sp", [(lfs_o[e, s, :, :], LF[0:16, s, :])], ("st", "lf"), reads=[bLF], is_out=True)
                for s in range(2):
                    prs = [(LFC[:, s, k, :], ONEf[:, 0:16]) for k in range(8)] + [(LF[0:16, s, :], TRI[0:16, 0:16])]
                    mm(bPB[6], PB[6][0:8, s * 16:(s + 1) * 16], prs, reads=[bLF, bLFC])
                    for k in range(8):
                        prs = [(ONEf[:], LFC[:, s, k2, :]) for k2 in range(k)] + [(TRI, LFC[:, s, k, :])]
                        mm(bPB[5], PB[5][:, (s * 9 + k) * 8:(s * 9 + k + 1) * 8], prs, reads=[bLFC])
                    prs = [(ONEf[:, 0:16], LFC[:, s, k2, :]) for k2 in range(8)] + [(TRI[0:16, 0:16], LF[0:16, s, :])]
                    mm(bPB[5], PB[5][0:16, (s * 9 + 8) * 8:(s * 9 + 9) * 8], prs, reads=[bLFC, bLF])
                cx.op("dve", lambda E: E.tensor_copy(out=LT[:, 0:32], in_=PB[6][0:8, 0:32]), reads=[bPB[6]], writes=[bLT])
                cx.op("dve", lambda E: E.tensor_scalar(out=NEGLS[:, :, :, :].rearrange("p s k h -> p (s k h)"), in0=PB[5][:, 0:144],
                                                       scalar1=-1.0, scalar2=None, op0=ALU.mult), reads=[bPB[5]], writes=[bNEGLS])
                ncol = 32
                b = ring("main", 4)

                def fnq(E, b=b):
                    ins = None
                    for hh in range(HA):
                        ins = E.matmul(PB[b][0:1, hh * 32:(hh + 1) * 32], IDf[0:8, hh:hh + 1], LT[:, 0:32], start=True, stop=True)
                    return ins
                cx.op("pe", fnq, reads=[bLT], writes=[bPB[b]])
                cx.op("act", lambda E, b=b: E.activation(out=LQS[0:1, :], in_=PB[b][0:1, 0:256], func=AF.Copy, scale=float(DH ** 0.5)),
                      reads=[bPB[b]], writes=[bLQ])
            for g in range(3):
                b = ring("main", 4)
                mm(bPB[b], PB[b][:, 0:ncol], [(SEL(g), LT[:, 0:ncol])], reads=[bLT])
                cx.op("act", lambda E, b=b, g=g: E.activation(out=LQ[:, g, 0:ncol], in_=PB[b][:, 0:ncol], func=AF.Copy,
                                                              scale=float(DH ** 0.5)), reads=[bPB[b]], writes=[bLQ])

        def mixer_ab(tile, l):
            e = l // 2
            n = tile["n"]
            kind = tile["kind"]
            tbl = tile["tblocks"]
            norm_to_h(tile, lambda c, s: aav(l, 0, c, s), lambda c, s: modv(l, 0, c, s))
            cx.stage("norm%d" % l)
            cx.alias(bU + bUC + bYS, [bKBF, bVBF, bKTT])
            fzb = []
            for which in range(2):
                for cb in range(2):
                    Wt, bw = wget()
                    for tb, (c0, m) in enumerate(tbl):
                        b = ring("main", 4)
                        mm(bPB[b], PB[b][0:m, :], [(H[:, kc, c0:c0 + m], Wt[:, kc, 0:512]) for kc in range(16)], reads=[bw] + bH)
                        i = ring("kvs", 2)
                        cx.op("act", lambda E, b=b, i=i, m=m: E.copy(out=KVS[i][0:m, :], in_=PB[b][0:m, :]),
                              reads=[bPB[b]], writes=[bKVS[i]])
                        dstb = KBF if which == 0 else VBF
                        bdst = bKBF if which == 0 else bVBF
                        cx.op("dve", lambda E, i=i, m=m, tb=tb, cb=cb, dstb=dstb: E.tensor_copy(
                            out=dstb[0:m, tb * 1024 + cb * 512:tb * 1024 + (cb + 1) * 512], in_=KVS[i][0:m, :]),
                            reads=[bKVS[i]], writes=[bdst])
                        if kind == "prompt":
                            oo = (kp_o if which == 0 else vp_o)[e, tile["t0"] + c0:tile["t0"] + c0 + m, cb * 512:(cb + 1) * 512]
                        else:
                            oo = (ks_o if which == 0 else vs_o)[e, tb, :, cb * 512:(cb + 1) * 512]
                        cx.dma("sp", [(oo, KVS[i][0:m, :])], ("st", "kvs", i), reads=[bKVS[i]], is_out=True)
                        if which == 1 and cb == 1:
                            mm(bPB[6], PB[6][0:m, tb * 8:(tb + 1) * 8],
                               [(H[:, kc, c0:c0 + m], Wt[:, kc, 512:520]) for kc in range(16)], reads=[bw] + bH)
                            fzb.append((PB[6][0:m, tb * 8:(tb + 1) * 8], bPB[6], m, tb))
            cx.stage("kv%d" % l)
            for tb, (c0, m) in enumerate(tbl):
                def fn(E, tb=tb, c0=c0, m=m):
                    ins = None
                    for h in range(HA):
                        ins = E.transpose(PTP[:, h * 128:h * 128 + m], KBF[0:m, tb * 1024 + h * 128:tb * 1024 + (h + 1) * 128], IDb[0:m, 0:m])
                    return ins
                cx.op("pe", fn, reads=[bKBF], writes=[bPTP])
                cx.op("act", lambda E, c0=c0, m=m: E.copy(
                    out=KTT.rearrange("p (h t) -> p h t", t=TT)[:, :, c0:c0 + m],
                    in_=PTP[:, :].rearrange("p (h t) -> p h t", t=128)[:, :, 0:m]), reads=[bPTP], writes=[bKTT])
            if kind == "prompt":
                ti = tile["i"]
                t0 = tile["t0"]
                bKTs[(e, ti)] = Buf("kts")
                bVs[(e, ti)] = Buf("vs")
                cx.dma("sp", [(KTs[e, :, :, t0:t0 + TT].rearrange("h d t -> d h t"), KTT.rearrange("p (h t) -> p h t", t=TT))],
                       ("st", "ktt"), reads=[bKTT], writes=[bKTs[(e, ti)]])
                cx.dma("sp", [(Vs[e, h, :, ti * 4:(ti + 1) * 4, :],
                               VBF.rearrange("p (tb c) -> p tb c", c=1024)[:, :, h * 128:(h + 1) * 128]) for h in range(HA)],
                       ("st", "vbf"), reads=[bVBF], writes=[bVs[(e, ti)]])
            else:
                cx.dma("sp", [(LFC[:, s, :, :], cf[e, s].rearrange("(k p) h -> p k h", p=128)) for s in range(2)],
                       ("ld", "lfc"), writes=[bLFC])
            cx.stage("ktr%d" % l)
            logf_and_L(tile, e, fzb)
            cx.stage("logf%d" % l)
            for cb in range(2):
                Wt, bw = wget()
                for c4 in range(4):
                    hh = cb * 4 + c4
                    b = ring("main", 4)
                    mm(bPB[b], PB[b][:, 0:n], [(Wt[:, kc, c4 * 128:(c4 + 1) * 128], H[:, kc, 0:n]) for kc in range(16)], reads=[bw] + bH)
                    cx.op("act", lambda E, b=b, hh=hh: E.copy(out=QT[:, hh, 0:n], in_=PB[b][:, 0:n]), reads=[bPB[b]], writes=[bQT[hh]])
            cx.stage("q%d" % l)
            for h in range(HA):
                pg = 32 * (h % 3)
                gq = h // 3
                if kind == "prompt":
                    ti = tile["i"]
                    nkb = 4 * (ti + 1)
                    for kt in range(ti + 1):
                        si = ring("kth", 4)
                        cx.dma("sp", [(KTH[si], KTs[e, h, :, kt * TT:(kt + 1) * TT])], ("ld", "kth", si),
                               reads=[bKTs[(e, kt)]], writes=[bKTH[si]])
                        cx.dma("sp", [(VH[si], Vs[e, h, :, kt * 4:(kt + 1) * 4, :])], ("ld", "vh", si),
                               reads=[bVs[(e, kt)]], writes=[bVH[si]])
                        for kb in range(4):
                            g = kt * 4 + kb
                            diag = (kt == ti)
                            qlo = kb * 128 if diag else 0
                            sb_ = [0, 1, 2, 3, 6][ring("st", 5)]
                            prs = [(KTH[si][:, kb * 128:(kb + 1) * 128], QT[:, h, qlo:TT]),
                                   (ONEb[pg:pg + 1, :], LQ[pg:pg + 1, gq, qlo:TT])]
                            mm(bPB[sb_], PB[sb_][:, qlo:TT], prs, reads=[bKTH[si], bQT[h], bLQ], start=True, stop=not diag)
                            if diag:
                                mm(bPB[sb_], PB[sb_][:, qlo:qlo + 128], [(IDb[:], NMb[:])], reads=[], start=False, stop=True)
                            pi = ring("pt", 3)
                            cx.op("act", lambda E, sb_=sb_, pi=pi, qlo=qlo, g=g, h=h: E.activation(
                                out=PT[pi][:, qlo:TT], in_=PB[sb_][:, qlo:TT], func=AF.Exp, bias=NEGL[:, e, g, h:h + 1], scale=float(SCALE)),
                                reads=[bPB[sb_], bNEGL[e]], writes=[bPT[pi]])
                            mm(bPB[4], PB[4][:, qlo:TT], [(VH[si][:, kb, :], PT[pi][:, qlo:TT])], reads=[bVH[si], bPT[pi]],
                               start=(g == 0), stop=(g == nkb - 1))
                            mm(bPB[5], PB[5][:, qlo:TT], [(ONEb[:], PT[pi][:, qlo:TT])], reads=[bPT[pi]],
                               start=(g == 0), stop=(g == nkb - 1))
                    i = ring("tmpf", 3)
                    cx.op("dve", lambda E, i=i: E.reciprocal(out=TMPF[i][:, :], in_=PB[5][:, :]), reads=[bPB[5]], writes=[bTMPF[i]])
                    cx.op("dve", lambda E, i=i, h=h: E.tensor_tensor(out=MOt[:, h, :], in0=PB[4][:, :], in1=TMPF[i][:, :], op=ALU.mult),
                          reads=[bPB[4], bTMPF[i]], writes=[bMO[h]])
                else:
                    for s in range(2):
                        ci = (h * 2 + s) % 2
                        cx.dma("pool", [(CK[ci], ck[e, s, :, h * 128:(h + 1) * 128].rearrange("(k p) d -> p k d", p=128))],
                               ("ld", "ck", ci), writes=[bCK[ci]])
                        cx.dma("pool", [(CVv[ci], cv[e, s, :, h * 128:(h + 1) * 128].rearrange("(k p) d -> p k d", p=128))],
                               ("ld", "cv", ci), writes=[bCV[ci]])

                        if h == 0 and s == 0:
                            cx.stage("sa%d" % l)

                        def fn(E, ci=ci):
                            ins = None
                            for k in range(8):
                                ins = E.transpose(PTP[:, k * 128:(k + 1) * 128], CK[ci][:, k, :], IDb[:])
                            return ins
                        cx.op("pe", fn, reads=[bCK[ci]], writes=[bPTP])
                        cx.op("act", lambda E, ci=ci: E.copy(out=KTC[ci], in_=PTP[:, :]), reads=[bPTP], writes=[bKTC[ci]])
                        if h == 0 and s == 0:
                            cx.stage("sb%d" % l)
                        sb_ = [0, 1, 2, 3, 6][ring("st", 5)]
                        q0 = s * 16
                        for k in range(8):
                            mm(bPB[sb_], PB[sb_][:, k * 16:(k + 1) * 16],
                               [(KTC[ci][:, k * 128:(k + 1) * 128], QT[:, h, q0:q0 + 16]),
                                (ONEb[0:1, :], LQS[0:1, h * 32 + q0:h * 32 + q0 + 16])], reads=[bKTC[ci], bQT[h], bLQ])
                        mm(bPB[sb_], PB[sb_][0:16, 128:144],
                           [(KTT.rearrange("p (h t) -> p h t", t=TT)[:, h, q0:q0 + 16], QT[:, h, q0:q0 + 16]),
                            (ONEb[0:1, 0:16], LQS[0:1, h * 32 + q0:h * 32 + q0 + 16]),
                            (IDb[0:16, 0:16], NMb[0:16, 0:16])], reads=[bKTT, bQT[h], bLQ])
                        if h == 0 and s == 0:
                            cx.stage("sc%d" % l)
                        pi = ring("pt", 3)
                        for k in range(8):
                            cx.op("act", lambda E, sb_=sb_, pi=pi, k=k, s=s, h=h: E.activation(
                                out=PT[pi][:, k * 16:(k + 1) * 16], in_=PB[sb_][:, k * 16:(k + 1) * 16], func=AF.Exp,
                                bias=NEGLS[:, s, k, h:h + 1], scale=float(SCALE)), reads=[bPB[sb_], bNEGLS], writes=[bPT[pi]])
                        cx.op("act", lambda E, sb_=sb_, pi=pi, s=s, h=h: E.activation(
                            out=PT[pi][0:16, 128:144], in_=PB[sb_][0:16, 128:144], func=AF.Exp,
                            bias=NEGLS[0:16, s, 8, h:h + 1], scale=float(SCALE)), reads=[bPB[sb_], bNEGLS], writes=[bPT[pi]])
                        if h == 0 and s == 0:
                            cx.stage("sd%d" % l)
                        prs = [(CVv[ci][:, k, :], PT[pi][:, k * 16:(k + 1) * 16]) for k in range(8)]
                        prs.append((VBF[0:16, s * 1024 + h * 128:s * 1024 + (h + 1) * 128], PT[pi][0:16, 128:144]))
                        mm(bPB[4], PB[4][:, q0:q0 + 16], prs, reads=[bCV[ci], bPT[pi], bVBF])
                        prs = [(ONEb[:], PT[pi][:, k * 16:(k + 1) * 16]) for k in range(8)]
                        prs.append((ONEb[0:16, :], PT[pi][0:16, 128:144]))
                        mm(bPB[5], PB[5][:, q0:q0 + 16], prs, reads=[bPT[pi]])
                        if h == 0 and s == 0:
                            cx.stage("se%d" % l)
                        if h == 0 and s == 1:
                            cx.stage("sf%d" % l)
                    i = ring("tmpf", 3)
                    cx.op("dve", lambda E, i=i: E.reciprocal(out=TMPF[i][:, 0:32], in_=PB[5][:, 0:32]), reads=[bPB[5]], writes=[bTMPF[i]])
                    cx.op("dve", lambda E, i=i, h=h: E.tensor_tensor(out=MOt[:, h, 0:32], in0=PB[4][:, 0:32], in1=TMPF[i][:, 0:32], op=ALU.mult),
                          reads=[bPB[4], bTMPF[i]], writes=[bMO[h]])
                    cx.stage("sh%d_%d" % (h, l))
            cx.stage("att%d" % l)
            cx.alias([bKBF, bVBF, bKTT], bU + bUC)
            dwo = P.off[("dww", e)][0]
            for j in range(4):
                Wt, bw = wget()
                for cc in range(2):
                    c = 2 * j + cc
                    ba = ring("main", 4)
                    mm(bPB[ba], PB[ba][:, 0:n], [(Wt[:, kc, cc * 128:(cc + 1) * 128], H[:, kc, 0:n]) for kc in range(16)], reads=[bw] + bH)
                    bb = ring("main", 4)
                    mm(bPB[bb], PB[bb][:, 0:n], [(Wt[:, kc, 256 + cc * 128:256 + (cc + 1) * 128], H[:, kc, 0:n]) for kc in range(16)], reads=[bw] + bH)
                    i = ring("tmpf", 3)
                    cx.op("act", lambda E, bb=bb, i=i: E.activation(out=TMPF[i][:, 0:n], in_=PB[bb][:, 0:n], func=AF.Sigmoid),
                          reads=[bPB[bb]], writes=[bTMPF[i]])
                    if c == 0:
                        cx.stage("ga%d" % l)
                    if kind == "prompt":
                        ui = ring("u", 3)
                        U = Uv(ui)
                        cx.op("dve", lambda E, U=U, c=c: E.tensor_copy(out=U[:, 0:30], in_=HALOB[:, e, c, :]), reads=[bHALOB[e][c]], writes=[bU[ui]])
                        cx.op("dve", lambda E, U=U, ba=ba, i=i: E.tensor_tensor(out=U[:, 30:30 + TT], in0=PB[ba][:, :], in1=TMPF[i][:, :], op=ALU.mult),
                              reads=[bPB[ba], bTMPF[i]], writes=[bU[ui]])
                        cx.op("dve", lambda E, U=U, c=c: E.tensor_copy(out=HALOB[:, e, c, :], in_=U[:, TT:TT + 30]), reads=[bU[ui]], writes=[bHALOB[e][c]])
                        segs = [(U, 0, TT, 0)]
                        ubufs = [bU[ui]]
                    else:
                        segs = []
                        ubufs = []
                        hbo = P.off[("histb", e)][0]
                        for s in range(2):
                            ui = ring("u", 3)
                            U = Uv(ui)
                            ho = hbo + (s * 8 + c) * 30
                            cx.op("dve", lambda E, U=U, ho=ho: E.tensor_copy(out=U[:, 0:30], in_=PRM[:, ho:ho + 30]), writes=[bU[ui]])
                            cx.op("dve", lambda E, U=U, ba=ba, i=i, s=s: E.tensor_tensor(
                                out=U[:, 30:46], in0=PB[ba][:, s * 16:(s + 1) * 16], in1=TMPF[i][:, s * 16:(s + 1) * 16], op=ALU.mult),
                                reads=[bPB[ba], bTMPF[i]], writes=[bU[ui]])
                            cx.op("dve", lambda E, U=U, c=c, s=s: E.tensor_copy(out=HALOB[:, e, c, :], in_=U[:, 16:46]) if s == 0 else
                                  E.tensor_copy(out=MEANB[:, c * 30:(c + 1) * 30], in_=U[:, 16:46]),
                                  reads=[bU[ui]], writes=[bHALOB[e][c] if s == 0 else bMEANB])
                            segs.append((U, s * 16, 16, s))
                            ubufs.append(bU[ui])
                    if c == 0:
                        cx.stage("gb%d" % l)
                    for (U, c0, w, s), bu_ in zip(segs, ubufs):
                        cx.op("dve", lambda E, U=U, c0=c0, w=w, c=c: E.tensor_scalar(
                            out=UCv(c)[:, c0:c0 + w], in0=U[:, 0:w], scalar1=PRM[:, dwo + c * 31:dwo + c * 31 + 1],
                            scalar2=pp(("dwb", e), c), op0=ALU.mult, op1=ALU.add), reads=[bu_], writes=[bUC[c]])
                        for jt in range(1, KB):
                            cx.op("dve", lambda E, U=U, c0=c0, w=w, c=c, jt=jt: E.scalar_tensor_tensor(
                                out=UCv(c)[:, c0:c0 + w], in0=U[:, jt:jt + w], scalar=PRM[:, dwo + c * 31 + jt:dwo + c * 31 + jt + 1],
                                in1=UCv(c)[:, c0:c0 + w], op0=ALU.mult, op1=ALU.add), reads=[bu_, bUC[c]], writes=[bUC[c]])
                    if c == 0:
                        cx.stage("gc%d" % l)
                    i2 = ring("sq", 2)
                    cx.op("act", lambda E, c=c, i2=i2: E.activation(out=SQ[i2][:, 0:n], in_=UCv(c)[:, 0:n], func=AF.Square),
                          reads=[bUC[c]], writes=[bSQ[i2]])
                    i3 = ring("ucb", 2)
                    cx.op("act", lambda E, c=c, i3=i3: E.copy(out=UCB[i3][:, 0:n], in_=UCv(c)[:, 0:n]), reads=[bUC[c]], writes=[bUCB[i3]])
                    mm(bPB[4], PB[4][:, 0:n], [(ONEm[:], UCB[i3][:, 0:n])], reads=[bUCB[i3]], start=(c == 0), stop=(c == 7))
                    mm(bPB[5], PB[5][:, 0:n], [(ONEm[:], SQ[i2][:, 0:n])], reads=[bSQ[i2]], start=(c == 0), stop=(c == 7))
            cx.stage("gd%d" % l)
            cx.op("dve", lambda E: E.tensor_copy(out=MEANB[:, 0:n], in_=PB[4][:, 0:n]), reads=[bPB[4], bMEANB], writes=[bMEANB]) if kind == "prompt" else \
                cx.op("dve", lambda E: E.tensor_copy(out=TMPF[0][:, 0:n], in_=PB[4][:, 0:n]), reads=[bPB[4]], writes=[bTMPF[0]])
            MB = MEANB if kind == "prompt" else TMPF[0]
            bMB = bMEANB if kind == "prompt" else bTMPF[0]
            cx.op("dve", lambda E: E.tensor_tensor(out=TMPF[1][:, 0:n], in0=MB[:, 0:n], in1=MB[:, 0:n], op=ALU.mult), reads=[bMB], writes=[bTMPF[1]])
            cx.op("dve", lambda E: E.tensor_tensor(out=TMPF[1][:, 0:n], in0=PB[5][:, 0:n], in1=TMPF[1][:, 0:n], op=ALU.subtract),
                  reads=[bPB[5], bTMPF[1]], writes=[bTMPF[1]])
            cx.op("act", lambda E: E.activation(out=RSTD[:, 0:n], in_=TMPF[1][:, 0:n], func=AF.Sqrt, bias=EPS_L, scale=1.0),
                  reads=[bTMPF[1]], writes=[bRSTD])
            cx.op("dve", lambda E: E.reciprocal(out=RSTD[:, 0:n], in_=RSTD[:, 0:n]), reads=[bRSTD], writes=[bRSTD])
            rr["tmpf"] = 2
            cx.stage("ge%d" % l)
            for c in range(8):
                cx.op("dve", lambda E, c=c: E.tensor_tensor(out=UCv(c)[:, 0:n], in0=UCv(c)[:, 0:n], in1=MB[:, 0:n], op=ALU.subtract),
                      reads=[bUC[c], bMB], writes=[bUC[c]])
                cx.op("dve", lambda E, c=c: E.scalar_tensor_tensor(out=UCv(c)[:, 0:n], in0=UCv(c)[:, 0:n], scalar=pp(("lng", e), c),
                                                                   in1=RSTD[:, 0:n], op0=ALU.mult, op1=ALU.mult),
                      reads=[bUC[c], bRSTD], writes=[bUC[c]])
                cx.op("act", lambda E, c=c: E.activation(out=MOt[:, 8 + c, 0:n], in_=UCv(c)[:, 0:n], func=AF.Silu,
                                                         bias=pp(("lnb", e), c), scale=1.0), reads=[bUC[c]], writes=[bMO[8 + c]])
            cx.stage("glu%d" % l)
            out_proj(tile, l)
            cx.stage("outp%d" % l)
            if (kind == "prompt" and tile["i"] == NT - 1) or kind == "sample":
                nseq = 1 if kind == "prompt" else 2
                for s in range(nseq):
                    b = ring("main", 4)

                    def fn(E, b=b, s=s):
                        ins = None
                        for c in range(8):
                            src = HALOB[:, e, c, :] if s == 0 else MEANB[:, c * 30:(c + 1) * 30]
                            ins = E.transpose(PB[b][0:30, c * 128:(c + 1) * 128], src, IDf)
                        return ins
                    b2 = ring("main", 4)

                    def fn2(E, b=b, b2=b2, s=s):
                        ins = None
                        for c in range(8):
                            src = HALOB[:, e, c, :] if s == 0 else MEANB[:, c * 30:(c + 1) * 30]
                            bk = b if c < 4 else b2
                            ins = E.transpose(PB[bk][0:30, (c % 4) * 128:(c % 4 + 1) * 128], src, IDf)
                        return ins
                    cx.op("pe", fn2, reads=[bb_ for bb_ in bHALOB[e]] + [bMEANB], writes=[bPB[b], bPB[b2]])
                    i = ring("kvs", 2)
                    i2 = ring("kvs", 2)
                    cx.op("act", lambda E, b=b, i=i: E.copy(out=KVS[i][0:30, :], in_=PB[b][0:30, :]), reads=[bPB[b]], writes=[bKVS[i]])
                    cx.op("act", lambda E, b2=b2, i2=i2: E.copy(out=KVS[i2][0:30, :], in_=PB[b2][0:30, :]), reads=[bPB[b2]], writes=[bKVS[i2]])
                    oo = cbp_o[e] if kind == "prompt" else cbs_o[e, s]
                    cx.dma("sp", [(oo[:, 0:512], KVS[i][0:30, :])], ("st", "kvs", i), reads=[bKVS[i]], is_out=True)
                    cx.dma("sp", [(oo[:, 512:1024], KVS[i2][0:30, :])], ("st", "kvs", i2), reads=[bKVS[i2]], is_out=True)
            cx.alias(bU + bUC, bYS)

        def mixer_c(tile, l):
            e = l // 2
            n = tile["n"]
            kind = tile["kind"]
            norm_to_h(tile, lambda c, s: aav(l, 0, c, s), lambda c, s: modv(l, 0, c, s))
            cx.alias([bKBF, bVBF, bKTT] + bYS, bU + bUC)
            cwo = P.off[("ccw", e)][0]
            for j in range(16):
                Wt, bw = wget()
                bks = []
                for part in range(3):
                    b = ring("main", 4)
                    mm(bPB[b], PB[b][:, 0:n], [(Wt[:, kc, part * 128:(part + 1) * 128], H[:, kc, 0:n]) for kc in range(16)], reads=[bw] + bH)
                    bks.append(b)
                bbg, bcg, bxv = bks
                i = ring("tmpf", 3)
                cx.op("act", lambda E, bcg=bcg, i=i: E.copy(out=TMPF[i][:, 0:n], in_=PB[bcg][:, 0:n]), reads=[bPB[bcg]], writes=[bTMPF[i]])
                if kind == "prompt":
                    ui = ring("u", 3)
                    U = Uv(ui)
                    cx.op("dve", lambda E, U=U, j=j: E.tensor_copy(out=U[:, 0:2], in_=HALOC[:, e, j, :]), reads=[bHALOC[e][j]], writes=[bU[ui]])
                    cx.op("dve", lambda E, U=U, bxv=bxv, i=i: E.tensor_tensor(out=U[:, 2:2 + TT], in0=PB[bxv][:, :], in1=TMPF[i][:, :], op=ALU.mult),
                          reads=[bPB[bxv], bTMPF[i]], writes=[bU[ui]])
                    cx.op("dve", lambda E, U=U, j=j: E.tensor_copy(out=HALOC[:, e, j, :], in_=U[:, TT:TT + 2]), reads=[bU[ui]], writes=[bHALOC[e][j]])
                    segs = [(U, 0, TT, bU[ui])]
                else:
                    segs = []
                    hco = P.off[("histc", e)][0]
                    for s in range(2):
                        ui = ring("u", 3)
                        U = Uv(ui)
                        ho = hco + (s * 16 + j) * 2
                        cx.op("dve", lambda E, U=U, ho=ho: E.tensor_copy(out=U[:, 0:2], in_=PRM[:, ho:ho + 2]), writes=[bU[ui]])
                        cx.op("dve", lambda E, U=U, bxv=bxv, i=i, s=s: E.tensor_tensor(
                            out=U[:, 2:18], in0=PB[bxv][:, s * 16:(s + 1) * 16], in1=TMPF[i][:, s * 16:(s + 1) * 16], op=ALU.mult),
                            reads=[bPB[bxv], bTMPF[i]], writes=[bU[ui]])
                        cx.op("dve", lambda E, U=U, j=j, s=s: E.tensor_copy(out=HALOC[:, e, j, :], in_=U[:, 16:18]) if s == 0 else
                              E.tensor_copy(out=MEANB[:, j * 2:(j + 1) * 2], in_=U[:, 16:18]),
                              reads=[bU[ui]], writes=[bHALOC[e][j] if s == 0 else bMEANB])
                        segs.append((U, s * 16, 16, bU[ui]))
                uci = j % 8
                for (U, c0, w, bu_) in segs:
                    cx.op("dve", lambda E, U=U, c0=c0, w=w, j=j, uci=uci: E.tensor_scalar(
                        out=UCv(uci)[:, c0:c0 + w], in0=U[:, 0:w], scalar1=PRM[:, cwo + j * 3:cwo + j * 3 + 1], scalar2=None, op0=ALU.mult),
                        reads=[bu_], writes=[bUC[uci]])
                    for jt in (1, 2):
                        cx.op("dve", lambda E, U=U, c0=c0, w=w, j=j, jt=jt, uci=uci: E.scalar_tensor_tensor(
                            out=UCv(uci)[:, c0:c0 + w], in0=U[:, jt:jt + w], scalar=PRM[:, cwo + j * 3 + jt:cwo + j * 3 + jt + 1],
                            in1=UCv(uci)[:, c0:c0 + w], op0=ALU.mult, op1=ALU.add), reads=[bu_, bUC[uci]], writes=[bUC[uci]])
                cx.op("dve", lambda E, bbg=bbg, j=j, uci=uci: E.tensor_tensor(out=MOt[:, j, 0:n], in0=PB[bbg][:, 0:n], in1=UCv(uci)[:, 0:n], op=ALU.mult),
                      reads=[bPB[bbg], bUC[uci]], writes=[bMO[j]])
            out_proj(tile, l)
            if (kind == "prompt" and tile["i"] == NT - 1) or kind == "sample":
                nseq = 1 if kind == "prompt" else 2
                for s in range(nseq):
                    bs_ = [ring("main", 4) for _ in range(4)]

                    def fn(E, bs_=bs_, s=s):
                        ins = None
                        for c in range(16):
                            src = HALOC[:, e, c, :] if s == 0 else MEANB[:, c * 2:(c + 1) * 2]
                            ins = E.transpose(PB[bs_[c // 4]][0:2, (c % 4) * 128:(c % 4 + 1) * 128], src, IDf)
                        return ins
                    cx.op("pe", fn, reads=bHALOC[e] + [bMEANB], writes=[bPB[b] for b in bs_])
                    oo = ccp_o[e] if kind == "prompt" else ccs_o[e, s]
                    for q, b in enumerate(bs_):
                        i = ring("kvs", 2)
                        cx.op("act", lambda E, b=b, i=i: E.copy(out=KVS[i][0:2, :], in_=PB[b][0:2, :]), reads=[bPB[b]], writes=[bKVS[i]])
                        cx.dma("sp", [(oo[:, q * 512:(q + 1) * 512], KVS[i][0:2, :])], ("st", "kvs", i), reads=[bKVS[i]], is_out=True)
            cx.alias(bU + bUC, bYS)

        def final_out(tile):
            n = tile["n"]
            kind = tile["kind"]
            go = P.off["gfin"][0]
            cx.alias(bU + bUC + [bKBF, bVBF, bKTT], bYS)
            norm_sq_only(tile)
            for tb, (c0, m) in enumerate(tile["yblocks"]):
                yi = ring("ys", 2)
                for c in range(NCH):
                    i = ring("tmpf", 3)
                    cx.op("dve", lambda E, c=c, c0=c0, m=m, i=i: E.scalar_tensor_tensor(
                        out=TMPF[i][:, 0:m], in0=X[:, c, c0:c0 + m], scalar=PRM[:, go + c:go + c + 1], in1=RSTD[:, c0:c0 + m],
                        op0=ALU.mult, op1=ALU.mult), reads=[bX[c], bRSTD], writes=[bTMPF[i]])
                    b = ring("main", 4)
                    cx.op("pe", lambda E, i=i, m=m, b=b: E.transpose(PB[b][0:m, 0:128], TMPF[i][:, 0:m], IDf), reads=[bTMPF[i]], writes=[bPB[b]])
                    cx.op("act", lambda E, b=b, m=m, c=c, yi=yi: E.copy(out=YSv(yi)[0:m, c * 128:(c + 1) * 128], in_=PB[b][0:m, 0:128]),
                          reads=[bPB[b]], writes=[bYS[yi]])
                oo = y_o[tile["t0"] + c0:tile["t0"] + c0 + m, :] if kind == "prompt" else ys_o[c0:c0 + m, :]
                cx.dma("sp", [(oo, YSv(yi)[0:m, :])], ("st", "ys", yi), reads=[bYS[yi]], is_out=True)

        def norm_sq_only(tile):
            n = tile["n"]
            for c in range(NCH):
                i = ring("sq", 2)
                cx.op("act", lambda E, c=c, i=i: E.activation(out=SQ[i][:, 0:n], in_=X[:, c, 0:n], func=AF.Square),
                      reads=[bX[c]], writes=[bSQ[i]])
                mm(bPB[4], PB[4][:, 0:n], [(ONEn[:], SQ[i][:, 0:n])], reads=[bSQ[i]], start=(c == 0), stop=(c == NCH - 1))
            cx.op("act", lambda E: E.activation(out=RSTD[:, 0:n], in_=PB[4][:, 0:n], func=AF.Sqrt, bias=EPS_R, scale=1.0),
                  reads=[bPB[4]], writes=[bRSTD])
            cx.op("dve", lambda E: E.reciprocal(out=RSTD[:, 0:n], in_=RSTD[:, 0:n]), reads=[bRSTD], writes=[bRSTD])

        tiles = []
        for i in range(NT):
            tiles.append(dict(kind="prompt", i=i, t0=i * TT, n=TT, segs=[(0, TT, 0)],
                              tblocks=[(k * 128, 128) for k in range(4)], yblocks=[(k * 128, 128) for k in range(4)]))
        if with_sample:
            tiles.append(dict(kind="sample", n=32, segs=[(0, 16, 1), (16, 16, 2)], tblocks=[(0, 16), (16, 16)], yblocks=[(0, 32)]))
        for pi_, tile in enumerate(tiles):
            if pi_ > 0:
                cx.new_pass()
            if tile["kind"] == "prompt":
                cx.dma("sp", [(X[:, :, :], xT[:, tile["t0"]:tile["t0"] + TT].rearrange("(c p) t -> p c t", p=128))], ("ld", "x"), writes=bX)
            else:
                cx.alias(bKTH + bVH, bCK + bCV + bKTC)
                cx.dma("sp", [(X[:, :, 0:32], xsT.rearrange("(c p) t -> p c t", p=128))], ("ld", "x"), writes=bX)
            for l in range(4):
                if l % 2 == 0:
                    mixer_ab(tile, l)
                else:
                    mixer_c(tile, l)
                cx.stage("mix%d" % l)
                ffn(tile, l)
                cx.stage("ffn%d" % l)
                if dbg and ("x_l%d_t%d" % (l, pi_)) in dbg_o:
                    cx.dma("sp", [(dbg_o["x_l%d_t%d" % (l, pi_)].rearrange("(c p) t -> p c t", p=128), X[:, :, 0:tile["n"]])],
                           ("st", "dbg"), reads=bX, is_out=True)
            final_out(tile)
        assert cx.kstop or ws["use"] == len(blocks), (ws["use"], len(blocks))
        cx.dead = False
        cx._waits("sp", cx.out_events)
        cx.ops["sp"].append(lambda E: E.nop())

        with nc.Block() as block:
            @block.tensor
            def _(E):
                for f in cx.ops["pe"]:
                    f(E)

            @block.scalar
            def _(E):
                for f in cx.ops["act"]:
                    f(E)

            @block.vector
            def _(E):
                for f in cx.ops["dve"]:
                    f(E)

            @block.gpsimd
            def _(E):
                for f in cx.ops["pool"]:
                    f(E)

            @block.sync
            def _(E):
                for f in cx.ops["sp"]:
                    f(E)
        counts = {e: len(cx.ops[e]) for e in ENG}
        print("instr counts", counts, "sems", len(cx.semh))
    return nc


def host_prep(inp, core, T):
    P = param_layout()
    prm = np.zeros((128, P.n), np.float32)

    def put(name, arr):
        o, w = P.off[name]
        arr = np.asarray(arr, np.float32).reshape(128, w)
        prm[:, o:o + w] = arr

    fm = lambda v: np.asarray(v, np.float32).reshape(-1, 128).T
    for l in range(4):
        put(("adab", l), fm(inp["ada_b"][l]))
        put(("gmix", l), fm(inp["norm_mix_g"][l]))
        put(("gffn", l), fm(inp["norm_ffn_g"][l]))
    put("gfin", fm(inp["final_g"]))
    for e in range(2):
        put(("bf", e), np.broadcast_to(inp["b_f"][e][None, :], (128, 8)))
        put(("dww", e), inp["dw_b_w"][e].reshape(KB, 8, 128).transpose(2, 1, 0))
        put(("dwb", e), fm(inp["dw_b_bias"][e]))
        put(("lng", e), fm(inp["ln_b_g"][e]))
        put(("lnb", e), fm(inp["ln_b_b"][e]))
        put(("ccw", e), inp["conv_c_w"][e].reshape(KC, 16, 128).transpose(2, 1, 0))
        hb = inp["state_convb"][e, 2 * core:2 * core + 2]
        put(("histb", e), hb.reshape(2, 30, 8, 128).transpose(3, 0, 2, 1))
        hc = inp["state_convc"][e, 2 * core:2 * core + 2]
        put(("histc", e), hc.reshape(2, 2, 16, 128).transpose(3, 0, 2, 1))
    cs = np.stack([inp["c_prompt"][core], inp["c_sample"][2 * core], inp["c_sample"][2 * core + 1]], 0)
    put("cT", cs.reshape(3, 16, 128).transpose(2, 1, 0))
    m = {
        "xT": np.ascontiguousarray(inp["x_prompt"][core, :T].T),
        "xsT": np.ascontiguousarray(inp["x_sample"][2 * core:2 * core + 2].reshape(32, D).T),
        "ck": np.ascontiguousarray(inp["cache_k"][:, 2 * core:2 * core + 2].reshape(2, 2, PAST, DA)),
        "cv": np.ascontiguousarray(inp["cache_v"][:, 2 * core:2 * core + 2].reshape(2, 2, PAST, DA)),
        "cf": np.ascontiguousarray(inp["cache_logf"][:, 2 * core:2 * core + 2]),
        "prm": prm,
        "cst": make_consts(),
    }
    for k in ("ada_w", "w_in_ab", "w_out_ab", "w_in_c", "w_out_c", "w_gate", "w_up", "w_down"):
        m[k] = np.asarray(inp[k], np.float32)
    return m


def assemble(results, T, ncores):
    B = ncores
    f = np.float32
    y_prompt = np.stack([results[c]["y"] for c in range(B)], 0)
    y_sample = np.concatenate([results[c]["ys"].reshape(2, SS, D) for c in range(B)], 0)
    k_p = np.stack([results[c]["k_p"].reshape(2, T, HA, DH) for c in range(B)], 1)
    v_p = np.stack([results[c]["v_p"].reshape(2, T, HA, DH) for c in range(B)], 1)
    logf_p = np.stack([results[c]["logf_p"] for c in range(B)], 1)
    convb_p = np.stack([results[c]["convb_p"] for c in range(B)], 1)
    convc_p = np.stack([results[c]["convc_p"] for c in range(B)], 1)
    k_s = np.concatenate([results[c]["k_s"].reshape(2, 2, SS, HA, DH) for c in range(B)], 1)
    v_s = np.concatenate([results[c]["v_s"].reshape(2, 2, SS, HA, DH) for c in range(B)], 1)
    logf_s = np.concatenate([results[c]["logf_s"] for c in range(B)], 1)
    convb_s = np.concatenate([results[c]["convb_s"] for c in range(B)], 1)
    convc_s = np.concatenate([results[c]["convc_s"] for c in range(B)], 1)
    outs = (y_prompt, y_sample, k_p, v_p, logf_p, convb_p, convc_p, k_s, v_s, logf_s, convb_s, convc_s)
    return tuple(np.ascontiguousarray(o.astype(f)) for o in outs)


def kernel(**inputs):
    inp = {k: np.asarray(v) for k, v in inputs.items()}
    T = inp["x_prompt"].shape[1]
    ncores = 8
    nc = build(T, with_sample=True)
    in_maps = [host_prep(inp, c, T) for c in range(ncores)]
    res = run_bass_kernel_spmd(nc, in_maps, core_ids=list(range(ncores)))
    return assemble(res.results, T, ncores)
```

```python
from contextlib import ExitStack
import os
import numpy as np
import concourse.bass as bass
import concourse.mybir as mybir
from concourse.bass_utils import run_bass_kernel_spmd

F32 = mybir.dt.float32
BF16 = mybir.dt.bfloat16
AF = mybir.ActivationFunctionType
ALU = mybir.AluOpType

D = 2048
NCH = 16
DFF = 5632
NFF = 44
HA = 8
DH = 128
DA = 1024
DB = 1024
KB = 31
KC = 3
DIN_AB = 5128
PAST = 1024
SS = 16
TT = 512
NEG = -30000.0
SCALE = DH ** -0.5
ENG = ("pe", "act", "dve", "pool", "sp")
NSLOT = 3
QSZ = (12, 10, 12, 10)
WCOLS = 520


class Buf:
    __slots__ = ("n", "w", "rs")

    def __init__(self, n):
        self.n = n
        self.w = None
        self.rs = {}


class Ctx:
    def __init__(self, nc, stack):
        self.nc = nc
        self.stack = stack
        self.ops = {e: [] for e in ENG}
        self.waited = {e: {} for e in ENG}
        self.semh = {}
        self.cnt = {}
        self.prog = {}
        self.passno = -1
        self.out_events = []
        self.dead = False
        self.kstop = os.environ.get("KSTOP", "")
        self.new_pass()

    def stage(self, name):
        if self.kstop and (name == self.kstop or ("p%d:%s" % (self.passno, name)) == self.kstop):
            self.dead = True

    def sem(self, key):
        if key not in self.semh:
            self.semh[key] = self.stack.enter_context(self.nc.semaphore("s%d" % len(self.semh)))
            self.cnt[key] = 0
        return key

    def new_pass(self):
        self.passno += 1
        for e in ("pe", "act", "dve", "pool"):
            self.prog[e] = self.sem(("prog", e, self.passno))

    def _collect(self, reads, writes):
        evs = []
        for b in reads:
            if b.w is not None:
                evs.append(b.w)
        for b in writes:
            if b.rs:
                evs.extend(b.rs.items())
            elif b.w is not None:
                evs.append(b.w)
        return evs

    def _waits(self, eng, evs):
        need = {}
        for (k, v) in evs:
            if eng == "pe" and k[0] == "prog" and k[1] == "pe":
                continue
            if need.get(k, 0) < v:
                need[k] = v
        wd = self.waited[eng]
        for k, v in need.items():
            if wd.get(k, 0) >= v:
                continue
            wd[k] = v
            h = self.semh[k]
            self.ops[eng].append(lambda E, h=h, v=v: E.wait_ge(h, v))

    def _update(self, ev, reads, writes):
        k, v = ev
        for b in reads:
            if b.rs.get(k, 0) < v:
                b.rs[k] = v
        for b in writes:
            b.w = ev
            b.rs = {}

    def op(self, eng, fn, reads=(), writes=()):
        if self.dead:
            return None
        self._waits(eng, self._collect(reads, writes))
        k = self.prog[eng]
        self.cnt[k] += 1
        ev = (k, self.cnt[k])
        h = self.semh[k]
        self.ops[eng].append(lambda E, fn=fn, h=h: fn(E).then_inc(h, 1))
        self._update(ev, reads, writes)
        return ev

    def dma(self, q, pairs, semkey, reads=(), writes=(), is_out=False):
        if self.dead:
            return None
        self._waits(q, self._collect(reads, writes))
        k = self.sem(semkey)
        h = self.semh[k]
        for (o, i) in pairs:
            self.cnt[k] += 16
            self.ops[q].append(lambda E, o=o, i=i, h=h: E.dma_start(out=o, in_=i).then_inc(h, 16))
        ev = (k, self.cnt[k])
        self._update(ev, reads, writes)
        if is_out:
            self.out_events.append(ev)
        return ev

    def alias(self, old, new):
        evs = {}
        for b in old:
            if b.w is not None and evs.get(b.w[0], 0) < b.w[1]:
                evs[b.w[0]] = b.w[1]
            for k, v in b.rs.items():
                if evs.get(k, 0) < v:
                    evs[k] = v
        for b in new:
            b.w = None
            b.rs = dict(evs)


class PAlloc:
    def __init__(self):
        self.off = {}
        self.n = 0

    def add(self, name, w):
        self.off[name] = (self.n, w)
        self.n += w


def param_layout():
    P = PAlloc()
    for l in range(4):
        P.add(("adab", l), 96)
        P.add(("gmix", l), 16)
        P.add(("gffn", l), 16)
    P.add("gfin", 16)
    for e in range(2):
        P.add(("bf", e), 8)
        P.add(("dww", e), 8 * 31)
        P.add(("dwb", e), 8)
        P.add(("lng", e), 8)
        P.add(("lnb", e), 8)
        P.add(("ccw", e), 16 * 3)
        P.add(("histb", e), 2 * 8 * 30)
        P.add(("histc", e), 2 * 16 * 2)
    P.add("cT", 16 * 3)
    return P


CONST_W = 128 * 3 + 384


def make_consts():
    c = np.zeros((128, CONST_W), np.float32)
    c[:, 0:128] = np.eye(128, dtype=np.float32)
    s = np.arange(128)
    c[:, 128:256] = (s[:, None] <= s[None, :]).astype(np.float32)
    c[:, 256:384] = np.where(s[:, None] <= s[None, :], 0.0, NEG)
    for k in range(8):
        g = k // 3
        c[k, 384 + g * 128 + 32 * (k % 3)] = 1.0
    return c


def build(T, with_sample=True, dbg=None):
    NT = T // TT
    nc = bass.Bass("TRN2", target_bir_lowering=False)
    P = param_layout()
    dt_in = lambda name, shape: nc.dram_tensor(name, list(shape), F32, kind="ExternalInput").ap()
    dt_out = lambda name, shape: nc.dram_tensor(name, list(shape), F32, kind="ExternalOutput").ap()
    xT = dt_in("xT", (D, T))
    xsT = dt_in("xsT", (D, 32))
    ck = dt_in("ck", (2, 2, PAST, DA))
    cv = dt_in("cv", (2, 2, PAST, DA))
    cf = dt_in("cf", (2, 2, PAST, HA))
    prm_d = dt_in("prm", (128, P.n))
    cst_d = dt_in("cst", (128, CONST_W))
    ada_w = dt_in("ada_w", (4, D, 6 * D))
    w_in_ab = dt_in("w_in_ab", (2, D, DIN_AB))
    w_out_ab = dt_in("w_out_ab", (2, D, D))
    w_in_c = dt_in("w_in_c", (2, D, 3 * D))
    w_out_c = dt_in("w_out_c", (2, D, D))
    w_gate = dt_in("w_gate", (4, D, DFF))
    w_up = dt_in("w_up", (4, D, DFF))
    w_down = dt_in("w_down", (4, DFF, D))
    y_o = dt_out("y", (T, D))
    ys_o = dt_out("ys", (32, D))
    kp_o = dt_out("k_p", (2, T, DA))
    vp_o = dt_out("v_p", (2, T, DA))
    lfp_o = dt_out("logf_p", (2, T, HA))
    cbp_o = dt_out("convb_p", (2, 30, DB))
    ccp_o = dt_out("convc_p", (2, 2, D))
    ks_o = dt_out("k_s", (2, 2, SS, DA))
    vs_o = dt_out("v_s", (2, 2, SS, DA))
    lfs_o = dt_out("logf_s", (2, 2, SS, HA))
    cbs_o = dt_out("convb_s", (2, 2, 30, DB))
    ccs_o = dt_out("convc_s", (2, 2, 2, D))
    NBLK = T // 128
    KTs = nc.dram_tensor("KTs", [2, HA, 128, T], BF16, kind="Internal").ap()
    Vs = nc.dram_tensor("Vs", [2, HA, 128, NBLK, 128], BF16, kind="Internal").ap()
    dbg_o = {}
    if dbg:
        for name, shape in dbg.items():
            dbg_o[name] = dt_out("dbg_" + name, shape)

    stack = ExitStack()
    with stack:
        sb = lambda name, shape, dt=F32: stack.enter_context(nc.sbuf_tensor(name, list(shape), dt))
        ps = lambda name, shape, dt=F32: stack.enter_context(nc.psum_tensor(name, list(shape), dt))
        cx = Ctx(nc, stack)

        PRM = sb("PRM", (128, P.n))
        CSTF = sb("CSTF", (128, CONST_W))
        IDb = sb("IDb", (128, 128), BF16)
        NMb = sb("NMb", (128, 128), BF16)
        ONEb = sb("ONEb", (128, 128), BF16)
        ONEn = sb("ONEn", (128, 128), BF16)
        ONEm = sb("ONEm", (128, 128), BF16)
        ONEf = sb("ONEf", (128, 128))
        CC = sb("CC", (128, 4))
        MOD = sb("MOD", (128, 4 * 6 * 16 * 3))
        AA = sb("AA", (128, 4 * 2 * 16 * 3))
        scT = sb("scT", (128, 16 * 3), BF16)
        X = sb("X", (128, NCH, TT))
        H = sb("H", (128, NCH, TT), BF16)
        WR = [sb("WR%d" % i, (128, 16, WCOLS), BF16) for i in range(NSLOT)]
        QT = sb("QT", (128, HA, TT), BF16)
        KVR = sb("KVR", (128, 4096), BF16)
        KTH = [KVR[:, i * 512:(i + 1) * 512] for i in range(4)]
        VH = [KVR[:, 2048 + i * 512:2048 + (i + 1) * 512].rearrange("p (a d) -> p a d", d=128) for i in range(4)]
        CK = [KVR[:, 0:1024].rearrange("p (a d) -> p a d", d=128)] * 2
        CVv = [KVR[:, 1024:2048].rearrange("p (a d) -> p a d", d=128)] * 2
        KTC = [KVR[:, 2048:3072]] * 2
        R1t = sb("R1t", (128, 6144))
        MOt = sb("MOt", (128, 16, TT), BF16)
        RSTD = sb("RSTD", (128, TT))
        TMPF = [sb("TMPF%d" % i, (128, TT)) for i in range(3)]
        SQ = [sb("SQ%d" % i, (128, TT), BF16) for i in range(2)]
        UCB = [sb("UCB%d" % i, (128, TT), BF16) for i in range(2)]
        PT = [sb("PT%d" % i, (128, TT), BF16) for i in range(3)]
        KVS = [sb("KVS%d" % i, (128, TT)) for i in range(2)]
        LF = sb("LF", (128, 4, 8))
        LFC = sb("LFC", (128, 2, 8, 8))
        DC = sb("DC", (8, 8))
        CCOL = sb("CCOL", (8, 2))
        LQ = sb("LQ", (128, 3, TT), BF16)
        LQS = sb("LQS", (1, 256), BF16)
        NEGL = sb("NEGL", (128, 2, max(NBLK, 1), 8))
        NEGLS = sb("NEGLS", (128, 2, 9, 8))
        HALOB = sb("HALOB", (128, 2, 8, 30))
        HALOC = sb("HALOC", (128, 2, 16, 2))
        MEANB = sb("MEANB", (128, TT))
        LT = MEANB[0:8, :]
        def Uv(i):
            return R1t[:, i * 544:i * 544 + 542]

        def UCv(c):
            return R1t[:, 1632 + c * TT:1632 + (c + 1) * TT]

        def YSv(i):
            return R1t[:, i * 2048:(i + 1) * 2048]

        KBF = R1t[:, 0:2048].bitcast(BF16)
        VBF = R1t[:, 2048:4096].bitcast(BF16)
        KTT = R1t[:, 4096:6144].bitcast(BF16)

        PB = [ps("PB%d" % i, (128, TT)) for i in range(7)]
        PTP = ps("PTP", (128, 1024), BF16)

        bX = [Buf("x%d" % c) for c in range(NCH)]
        bH = [Buf("h%d" % c) for c in range(NCH)]
        bWR = [Buf("wr%d" % i) for i in range(NSLOT)]
        bQT = [Buf("qt%d" % h) for h in range(HA)]
        bKTH = [Buf("kth%d" % i) for i in range(4)]
        bVH = [Buf("vh%d" % i) for i in range(4)]
        bU = [Buf("u%d" % i) for i in range(3)]
        bUC = [Buf("uc%d" % c) for c in range(8)]
        bYS = [Buf("ys%d" % i) for i in range(2)]
        bKBF = Buf("kbf")
        bVBF = Buf("vbf")
        bKTT = Buf("ktt")
        bMO = [Buf("mo%d" % c) for c in range(16)]
        bRSTD = Buf("rstd")
        bTMPF = [Buf("tmpf%d" % i) for i in range(3)]
        bSQ = [Buf("sq%d" % i) for i in range(2)]
        bUCB = [Buf("ucb%d" % i) for i in range(2)]
        bPT = [Buf("pt%d" % i) for i in range(3)]
        bKVS = [Buf("kvs%d" % i) for i in range(2)]
        bLF = Buf("lf")
        bLFC = Buf("lfc")
        bDC = Buf("dc")
        bCCOL = [Buf("ccol0"), Buf("ccol1")]
        bLQ = Buf("lq")
        bNEGL = [Buf("negl0"), Buf("negl1")]
        bNEGLS = Buf("negls")
        bHALOB = [[Buf("hb%d_%d" % (e, c)) for c in range(8)] for e in range(2)]
        bHALOC = [[Buf("hc%d_%d" % (e, c)) for c in range(16)] for e in range(2)]
        bMEANB = Buf("meanb")
        bLT = bMEANB
        bPB = [Buf("pb%d" % i) for i in range(7)]
        bPTP = Buf("ptp")
        bCK = [Buf("ck0")] * 2
        bCV = [Buf("cv0")] * 2
        bKTC = [Buf("ktc0")] * 2
        bKTs = {}
        bVs = {}
        bMOD = Buf("mod")
        rr = {"tmpf": 0, "sq": 0, "ucb": 0, "pt": 0, "kvs": 0, "main": 0, "u": 0, "kth": 0, "st": 0, "ys": 0}

        def ring(name, n):
            i = rr[name] % n
            rr[name] += 1
            return i

        def pp(name, j=0, w=1):
            o, _ = P.off[name]
            return PRM[:, o + j:o + j + w]

        ev_prm = cx.dma("sp", [(PRM[:], prm_d)], ("ld", "prm"))
        ev_cst = cx.dma("sp", [(CSTF[:], cst_d)], ("ld", "cst"))
        bCST = Buf("cst")
        bCST.w = ev_cst
        bPRM = Buf("prm")
        bPRM.w = ev_prm
        bC2 = Buf("c2")
        cx.op("dve", lambda E: E.tensor_copy(out=IDb[:], in_=CSTF[:, 0:128]), reads=[bCST], writes=[bC2])
        cx.op("dve", lambda E: E.tensor_copy(out=NMb[:], in_=CSTF[:, 256:384]), reads=[bCST], writes=[bC2])
        cx.op("dve", lambda E: E.memset(ONEb[:], 1.0), writes=[bC2])
        cx.op("dve", lambda E: E.memset(ONEn[:], 1.0 / D), writes=[bC2])
        cx.op("dve", lambda E: E.memset(ONEm[:], 1.0 / DB), writes=[bC2])
        cx.op("dve", lambda E: E.memset(ONEf[:], 1.0), writes=[bC2])
        cx.op("dve", lambda E: E.memset(CC[:, 0:1], 0.0), writes=[bC2])
        cx.op("dve", lambda E: E.memset(CC[:, 1:2], 1.0), writes=[bC2])
        cx.op("dve", lambda E: E.memset(CC[:, 2:3], 1e-6), writes=[bC2])
        cx.op("dve", lambda E: E.memset(CC[:, 3:4], 1e-5), writes=[bC2])
        cx.op("dve", lambda E: E.memset(CCOL[:], 0.0), writes=bCCOL)
        cx.op("dve", lambda E: E.memset(HALOB[:], 0.0), writes=[b for r in bHALOB for b in r])
        evc = cx.op("dve", lambda E: E.memset(HALOC[:], 0.0), writes=[b for r in bHALOC for b in r])
        for e in ("pe", "act", "pool"):
            cx._waits(e, [evc, ev_prm, ev_cst])
        cx._waits("dve", [ev_prm, ev_cst])
        IDf = CSTF[:, 0:128]
        TRI = CSTF[:, 128:256]
        SEL = lambda g: CSTF[0:8, 384 + g * 128:384 + (g + 1) * 128]
        ZERO = CC[:, 0:1]
        ONE = CC[:, 1:2]
        EPS_R = CC[:, 2:3]
        EPS_L = CC[:, 3:4]

        cx.stage("consts")
        def wsrc(Wap, r0, nk, c0, n):
            return Wap[r0 * 128:(r0 + nk) * 128, c0:c0 + n].rearrange("(kc p) n -> p kc n", p=128)

        def layer_blocks(l):
            e = l // 2
            bl = []
            if l % 2 == 0:
                W = w_in_ab[e]
                bl.append([(16, 0, 512, wsrc(W, 0, 16, 1024, 512))])
                bl.append([(16, 0, 512, wsrc(W, 0, 16, 1536, 512))])
                bl.append([(16, 0, 512, wsrc(W, 0, 16, 2048, 512))])
                bl.append([(16, 0, 512, wsrc(W, 0, 16, 2560, 512)),
                           (16, 512, 8, wsrc(W, 0, 16, 3072, 8))])
                bl.append([(16, 0, 512, wsrc(W, 0, 16, 0, 512))])
                bl.append([(16, 0, 512, wsrc(W, 0, 16, 512, 512))])
                for j in range(4):
                    bl.append([(16, 0, 256, wsrc(W, 0, 16, 3080 + j * 256, 256)),
                               (16, 256, 256, wsrc(W, 0, 16, 4104 + j * 256, 256))])
                Wo = w_out_ab[e]
            else:
                W = w_in_c[e]
                for j in range(16):
                    bl.append([(16, 0, 128, wsrc(W, 0, 16, j * 128, 128)),
                               (16, 128, 128, wsrc(W, 0, 16, 2048 + j * 128, 128)),
                               (16, 256, 128, wsrc(W, 0, 16, 4096 + j * 128, 128))])
                Wo = w_out_c[e]
            for cb in range(4):
                bl.append([(16, 0, 512, wsrc(Wo, 0, 16, cb * 512, 512))])
            qoff = 0
            for qs in QSZ:
                for jb in range(qs // 2):
                    c0 = qoff + 2 * jb
                    bl.append([(16, 0, 256, wsrc(w_gate[l], 0, 16, c0 * 128, 256)),
                               (16, 256, 256, wsrc(w_up[l], 0, 16, c0 * 128, 256))])
                for cb in range(4):
                    bl.append([(qs, 0, 512, wsrc(w_down[l], qoff, qs, cb * 512, 512))])
                qoff += qs
            return bl

        n_pass = NT + (1 if with_sample else 0)
        blocks = []
        for l in range(4):
            for j in range(24):
                blocks.append([(16, 0, 512, wsrc(ada_w[l], 0, 16, j * 512, 512))])
        per_pass = [b for l in range(4) for b in layer_blocks(l)]
        for _ in range(n_pass):
            blocks.extend(per_pass)
        ws = {"load": 0, "use": 0}
        NADA = 96
        NPB = len(per_pass)
        WCH = 48
        Wscr_l = [nc.dram_tensor("Wscr%d" % k, [WCH, 128, 16, WCOLS], BF16, kind="Internal").ap()
                  for k in range((NPB + WCH - 1) // WCH)]

        class _W:
            def __getitem__(self, key):
                pj = key[0]
                return Wscr_l[pj // WCH][(pj % WCH,) + tuple(key[1:])]
        Wscr = _W()
        bWscr = {}

        def blk_ext(j):
            nk = max(p[0] for p in blocks[j])
            ncols = max(p[1] + p[2] for p in blocks[j])
            return nk, ncols

        def wget():
            i = ws["use"]
            ws["use"] += 1
            if cx.dead:
                return WR[i % NSLOT], bWR[i % NSLOT]
            jp = i - 1
            if n_pass > 1 and jp >= NADA and (jp - NADA) < NPB:
                pj = jp - NADA
                sp_ = jp % NSLOT
                nk, ncols = blk_ext(jp)
                bWscr[pj] = Buf("wscr%d" % pj)
                cx.dma("sp", [(Wscr[pj, :, 0:nk, 0:ncols], WR[sp_][:, 0:nk, 0:ncols])], ("wst", sp_),
                       reads=[bWR[sp_]], writes=[bWscr[pj]])
            while ws["load"] < min(len(blocks), i + NSLOT):
                j = ws["load"]
                s = j % NSLOT
                if j >= NADA + NPB:
                    pj = (j - NADA) % NPB
                    nk, ncols = blk_ext(j)
                    cx.dma("pool", [(WR[s][:, 0:nk, 0:ncols], Wscr[pj, :, 0:nk, 0:ncols])], ("w", s),
                           reads=[bWscr[pj]], writes=[bWR[s]])
                else:
                    pairs = [(WR[s][:, 0:nk, c0:c0 + n], src) for (nk, c0, n, src) in blocks[j]]
                    cx.dma("pool", pairs, ("w", s), writes=[bWR[s]])
                ws["load"] += 1
            return WR[i % NSLOT], bWR[i % NSLOT]

        def mm(bank, out_ap, pairs, reads, start=True, stop=True):
            def fn(E, out_ap=out_ap, pairs=pairs, start=start, stop=stop):
                n = len(pairs)
                ins = None
                for i, (l_, r_) in enumerate(pairs):
                    ins = E.matmul(out_ap, l_, r_, start=(start and i == 0), stop=(stop and i == n - 1))
                return ins
            return cx.op("pe", fn, reads=reads, writes=[bank])

        cTo = P.off["cT"][0]
        cx.op("act", lambda E: E.activation(out=scT[:], in_=PRM[:, cTo:cTo + 48], func=AF.Silu), writes=[bMOD])
        bSCT = Buf("sct")
        bSCT.w = bMOD.w
        for l in range(4):
            for j in range(24):
                Wt, bw = wget()
                kind, cg = j // 4, (j % 4) * 4
                for c4 in range(4):
                    b = ring("main", 4)
                    mm(bPB[b], PB[b][:, 0:3],
                       [(Wt[:, kc, c4 * 128:(c4 + 1) * 128], scT[:, kc * 3:kc * 3 + 3]) for kc in range(16)],
                       reads=[bw, bSCT])
                    mo = ((l * 6 + kind) * 16 + cg + c4) * 3
                    cx.op("dve", lambda E, b=b, mo=mo, l=l, jj=kind * 16 + cg + c4: E.tensor_scalar(
                        out=MOD[:, mo:mo + 3], in0=PB[b][:, 0:3], scalar1=pp(("adab", l), jj), scalar2=None,
                        op0=ALU.add), reads=[bPB[b]], writes=[bMOD])

        cx.stage("mods")

        def modv(l, kind, c, seq):
            o = ((l * 6 + kind) * 16 + c) * 3 + seq
            return MOD[:, o:o + 1]

        def aav(l, which, c, seq):
            o = ((l * 2 + which) * 16 + c) * 3 + seq
            return AA[:, o:o + 1]

        for l in range(4):
            for which, (gname, kind) in enumerate(((("gmix", l), 1), (("gffn", l), 4))):
                for c in range(16):
                    mo = ((l * 6 + kind) * 16 + c) * 3
                    ao = ((l * 2 + which) * 16 + c) * 3
                    cx.op("dve", lambda E, mo=mo, ao=ao, gname=gname, c=c: E.tensor_scalar(
                        out=AA[:, ao:ao + 3], in0=MOD[:, mo:mo + 3], scalar1=1.0, scalar2=pp(gname, c),
                        op0=ALU.add, op1=ALU.mult), reads=[bMOD], writes=[bMOD])
        if not cx.dead:
            for e in ("pe", "act", "pool"):
                cx._waits(e, [bMOD.w])

        def norm_to_h(tile, Acol, Bcol, to_f32=None):
            n = tile["n"]
            for c in range(NCH):
                i = ring("sq", 2)
                cx.op("act", lambda E, c=c, i=i: E.activation(out=SQ[i][:, 0:n], in_=X[:, c, 0:n], func=AF.Square),
                      reads=[bX[c]], writes=[bSQ[i]])
                mm(bPB[4], PB[4][:, 0:n], [(ONEn[:], SQ[i][:, 0:n])], reads=[bSQ[i]], start=(c == 0), stop=(c == NCH - 1))
            cx.op("act", lambda E: E.activation(out=RSTD[:, 0:n], in_=PB[4][:, 0:n], func=AF.Sqrt, bias=EPS_R, scale=1.0),
                  reads=[bPB[4]], writes=[bRSTD])
            cx.op("dve", lambda E: E.reciprocal(out=RSTD[:, 0:n], in_=RSTD[:, 0:n]), reads=[bRSTD], writes=[bRSTD])
            for c in range(NCH):
                for (c0, w, seq) in tile["segs"]:
                    if to_f32 is not None:
                        dst, bd = to_f32(c)
                        cx.op("dve", lambda E, c=c, c0=c0, w=w, seq=seq, dst=dst: E.scalar_tensor_tensor(
                            out=dst[:, c0:c0 + w], in0=X[:, c, c0:c0 + w], scalar=Acol(c, seq), in1=RSTD[:, c0:c0 + w],
                            op0=ALU.mult, op1=ALU.mult), reads=[bX[c], bRSTD], writes=[bd])
                    else:
                        i = ring("tmpf", 3)
                        cx.op("dve", lambda E, c=c, c0=c0, w=w, seq=seq, i=i: E.scalar_tensor_tensor(
                            out=TMPF[i][:, c0:c0 + w], in0=X[:, c, c0:c0 + w], scalar=Acol(c, seq), in1=RSTD[:, c0:c0 + w],
                            op0=ALU.mult, op1=ALU.mult), reads=[bX[c], bRSTD], writes=[bTMPF[i]])
                        cx.op("act", lambda E, c=c, c0=c0, w=w, seq=seq, i=i: E.activation(
                            out=H[:, c, c0:c0 + w], in_=TMPF[i][:, c0:c0 + w], func=AF.Identity, bias=Bcol(c, seq), scale=1.0),
                            reads=[bTMPF[i]], writes=[bH[c]])

        def resid_add(tile, oc, b, l, kind):
            for (c0, w, seq) in tile["segs"]:
                cx.op("dve", lambda E, c0=c0, w=w, seq=seq: E.scalar_tensor_tensor(
                    out=X[:, oc, c0:c0 + w], in0=PB[b][:, c0:c0 + w], scalar=modv(l, kind, oc, seq), in1=X[:, oc, c0:c0 + w],
                    op0=ALU.mult, op1=ALU.add), reads=[bPB[b], bX[oc]], writes=[bX[oc]])

        def out_proj(tile, l):
            n = tile["n"]
            for cb in range(4):
                Wt, bw = wget()
                for c4 in range(4):
                    oc = cb * 4 + c4
                    b = ring("main", 4)
                    mm(bPB[b], PB[b][:, 0:n],
                       [(Wt[:, kc, c4 * 128:(c4 + 1) * 128], MOt[:, kc, 0:n]) for kc in range(16)],
                       reads=[bw] + bMO[0:16])
                    resid_add(tile, oc, b, l, 2)

        def ffn(tile, l):
            n = tile["n"]
            norm_to_h(tile, lambda c, s: aav(l, 1, c, s), lambda c, s: modv(l, 3, c, s))
            for qs in QSZ:
                for jb in range(qs // 2):
                    Wt, bw = wget()
                    for cc in range(2):
                        hc = 2 * jb + cc
                        bg = ring("main", 4)
                        mm(bPB[bg], PB[bg][:, 0:n],
                           [(Wt[:, kc, cc * 128:(cc + 1) * 128], H[:, kc, 0:n]) for kc in range(16)], reads=[bw] + bH)
                        bu = ring("main", 4)
                        mm(bPB[bu], PB[bu][:, 0:n],
                           [(Wt[:, kc, 256 + cc * 128:256 + (cc + 1) * 128], H[:, kc, 0:n]) for kc in range(16)], reads=[bw] + bH)
                        i = ring("tmpf", 3)
                        cx.op("act", lambda E, bg=bg, i=i: E.activation(out=TMPF[i][:, 0:n], in_=PB[bg][:, 0:n], func=AF.Silu),
                              reads=[bPB[bg]], writes=[bTMPF[i]])
                        cx.op("dve", lambda E, bu=bu, i=i, hc=hc: E.tensor_tensor(
                            out=MOt[:, hc, 0:n], in0=PB[bu][:, 0:n], in1=TMPF[i][:, 0:n], op=ALU.mult),
                            reads=[bPB[bu], bTMPF[i]], writes=[bMO[hc]])
                for cb in range(4):
                    Wt, bw = wget()
                    for c4 in range(4):
                        b = ring("main", 4)
                        mm(bPB[b], PB[b][:, 0:n],
                           [(Wt[:, kk, c4 * 128:(c4 + 1) * 128], MOt[:, kk, 0:n]) for kk in range(qs)],
                           reads=[bw] + bMO[0:qs])
                        resid_add(tile, cb * 4 + c4, b, l, 5)

        def logf_and_L(tile, e, fzbanks):
            kind = tile["kind"]
            tbl = tile["tblocks"]
            for (pap, bb, m, tb) in fzbanks:
                cx.op("dve", lambda E, pap=pap, m=m, tb=tb: E.tensor_tensor(
                    out=LF[0:m, tb, :], in0=pap, in1=pp(("bf", e), 0, 8)[0:m, :], op=ALU.add), reads=[bb], writes=[bLF])
            nb = len(tbl)
            m0 = tbl[0][1]
            cx.op("act", lambda E: E.activation(out=LF[0:m0, 0:nb, :], in_=LF[0:m0, 0:nb, :], func=AF.Exp, scale=-1.0),
                  reads=[bLF], writes=[bLF])
            cx.op("act", lambda E: E.activation(out=LF[0:m0, 0:nb, :], in_=LF[0:m0, 0:nb, :], func=AF.Ln, bias=ONE[0:m0, :], scale=1.0),
                  reads=[bLF], writes=[bLF])
            cx.op("dve", lambda E: E.tensor_scalar(out=LF[0:m0, 0:nb, :], in0=LF[0:m0, 0:nb, :], scalar1=-1.0, scalar2=None, op0=ALU.mult),
                  reads=[bLF], writes=[bLF])
            if kind == "prompt":
                t0 = tile["t0"]
                ev = cx.dma("sp", [(lfp_o[e, t0:t0 + TT, :].rearrange("(tb p) h -> p tb h", p=128), LF[:, :, :])],
                            ("st", "lf"), reads=[bLF], is_out=True)
                cx.op("dve", lambda E: E.tensor_scalar(out=DC[:], in0=IDf[0:8, 0:8], scalar1=CCOL[:, e:e + 1], scalar2=None,
                                                       op0=ALU.mult), reads=[bCCOL[e]], writes=[bDC])
                g0 = t0 // 128
                for tb in range(4):
                    prs = [(LF[:, t2, :], ONEf[:]) for t2 in range(tb)] + [(LF[:, tb, :], TRI)]
                    mm(bPB[6], PB[6][0:8, tb * 128:(tb + 1) * 128], prs, reads=[bLF], start=True, stop=True)
                    prs = [(ONEf[:], LF[:, t2, :]) for t2 in range(tb)] + [(TRI, LF[:, tb, :]), (ONEf[0:8, :], DC[:])]
                    mm(bPB[5], PB[5][:, tb * 8:(tb + 1) * 8], prs, reads=[bLF, bDC])
                cx.op("dve", lambda E: E.tensor_scalar(out=LT[:, :], in0=PB[6][0:8, :], scalar1=CCOL[:, e:e + 1], scalar2=None,
                                                       op0=ALU.add), reads=[bPB[6], bCCOL[e]], writes=[bLT])
                cx.op("dve", lambda E: E.tensor_scalar(out=NEGL[:, e, g0:g0 + 4, :], in0=PB[5][:, 0:32].rearrange("p (a b) -> p a b", b=8),
                                                       scalar1=-1.0, scalar2=None, op0=ALU.mult), reads=[bPB[5]], writes=[bNEGL[e]])
                cx.op("dve", lambda E: E.tensor_copy(out=CCOL[:, e:e + 1], in_=LT[:, TT - 1:TT]), reads=[bLT], writes=[bCCOL[e]])
                ncol = TT
            else:
                for s in range(2):
                    cx.dma("sp", [(lfs_o[e, s, :, :], LF[0:16, s, :])], ("st", "lf"), reads=[bLF], is_out=True)
                for s in range(2):
                    prs = [(LFC[:, s, k, :], ONEf[:, 0:16]) for k in range(8)] + [(LF[0:16, s, :], TRI[0:16, 0:16])]
                    mm(bPB[6], PB[6][0:8, s * 16:(s + 1) * 16], prs, reads=[bLF, bLFC])
                    for k in range(8):
                        prs = [(ONEf[:], LFC[:, s, k2, :]) for k2 in range(k)] + [(TRI, LFC[:, s, k, :])]
                        mm(bPB[5], PB[5][:, (s * 9 + k) * 8:(s * 9 + k + 1) * 8], prs, reads=[bLFC])
                    prs = [(ONEf[:, 0:16], LFC[:, s, k2, :]) for k2 in range(8)] + [(TRI[0:16, 0:16], LF[0:16, s, :])]
                    mm(bPB[5], PB[5][0:16, (s * 9 + 8) * 8:(s * 9 + 9) * 8], prs, reads=[bLFC, bLF])
                cx.op("dve", lambda E: E.tensor_copy(out=LT[:, 0:32], in_=PB[6][0:8, 0:32]), reads=[bPB[6]], writes=[bLT])
                cx.op("dve", lambda E: E.tensor_scalar(out=NEGLS[:, :, :, :].rearrange("p s k h -> p (s k h)"), in0=PB[5][:, 0:144],
                                                       scalar1=-1.0, scalar2=None, op0=ALU.mult), reads=[bPB[5]], writes=[bNEGLS])
                ncol = 32
                b = ring("main", 4)

                def fnq(E, b=b):
                    ins = None
                    for hh in range(HA):
                        ins = E.matmul(PB[b][0:1, hh * 32:(hh + 1) * 32], IDf[0:8, hh:hh + 1], LT[:, 0:32], start=True, stop=True)
                    return ins
                cx.op("pe", fnq, reads=[bLT], writes=[bPB[b]])
                cx.op("act", lambda E, b=b: E.activation(out=LQS[0:1, :], in_=PB[b][0:1, 0:256], func=AF.Copy, scale=float(DH ** 0.5)),
                      reads=[bPB[b]], writes=[bLQ])
            for g in range(3):
                b = ring("main", 4)
                mm(bPB[b], PB[b][:, 0:ncol], [(SEL(g), LT[:, 0:ncol])], reads=[bLT])
                cx.op("act", lambda E, b=b, g=g: E.activation(out=LQ[:, g, 0:ncol], in_=PB[b][:, 0:ncol], func=AF.Copy,
                                                              scale=float(DH ** 0.5)), reads=[bPB[b]], writes=[bLQ])

        def mixer_ab(tile, l):
            e = l // 2
            n = tile["n"]
            kind = tile["kind"]
            tbl = tile["tblocks"]
            norm_to_h(tile, lambda c, s: aav(l, 0, c, s), lambda c, s: modv(l, 0, c, s))
            cx.stage("norm%d" % l)
            cx.alias(bU + bUC + bYS, [bKBF, bVBF, bKTT])
            fzb = []
            for which in range(2):
                for cb in range(2):
                    Wt, bw = wget()
                    for tb, (c0, m) in enumerate(tbl):
                        b = ring("main", 4)
                        mm(bPB[b], PB[b][0:m, :], [(H[:, kc, c0:c0 + m], Wt[:, kc, 0:512]) for kc in range(16)], reads=[bw] + bH)
                        i = ring("kvs", 2)
                        cx.op("act", lambda E, b=b, i=i, m=m: E.copy(out=KVS[i][0:m, :], in_=PB[b][0:m, :]),
                              reads=[bPB[b]], writes=[bKVS[i]])
                        dstb = KBF if which == 0 else VBF
                        bdst = bKBF if which == 0 else bVBF
                        cx.op("dve", lambda E, i=i, m=m, tb=tb, cb=cb, dstb=dstb: E.tensor_copy(
                            out=dstb[0:m, tb * 1024 + cb * 512:tb * 1024 + (cb + 1) * 512], in_=KVS[i][0:m, :]),
                            reads=[bKVS[i]], writes=[bdst])
                        if kind == "prompt":
                            oo = (kp_o if which == 0 else vp_o)[e, tile["t0"] + c0:tile["t0"] + c0 + m, cb * 512:(cb + 1) * 512]
                        else:
                            oo = (ks_o if which == 0 else vs_o)[e, tb, :, cb * 512:(cb + 1) * 512]
                        cx.dma("sp", [(oo, KVS[i][0:m, :])], ("st", "kvs", i), reads=[bKVS[i]], is_out=True)
                        if which == 1 and cb == 1:
                            mm(bPB[6], PB[6][0:m, tb * 8:(tb + 1) * 8],
                               [(H[:, kc, c0:c0 + m], Wt[:, kc, 512:520]) for kc in range(16)], reads=[bw] + bH)
                            fzb.append((PB[6][0:m, tb * 8:(tb + 1) * 8], bPB[6], m, tb))
            cx.stage("kv%d" % l)
            for tb, (c0, m) in enumerate(tbl):
                def fn(E, tb=tb, c0=c0, m=m):
                    ins = None
                    for h in range(HA):
                        ins = E.transpose(PTP[:, h * 128:h * 128 + m], KBF[0:m, tb * 1024 + h * 128:tb * 1024 + (h + 1) * 128], IDb[0:m, 0:m])
                    return ins
                cx.op("pe", fn, reads=[bKBF], writes=[bPTP])
                cx.op("act", lambda E, c0=c0, m=m: E.copy(
                    out=KTT.rearrange("p (h t) -> p h t", t=TT)[:, :, c0:c0 + m],
                    in_=PTP[:, :].rearrange("p (h t) -> p h t", t=128)[:, :, 0:m]), reads=[bPTP], writes=[bKTT])
            if kind == "prompt":
                ti = tile["i"]
                t0 = tile["t0"]
                bKTs[(e, ti)] = Buf("kts")
                bVs[(e, ti)] = Buf("vs")
                cx.dma("sp", [(KTs[e, :, :, t0:t0 + TT].rearrange("h d t -> d h t"), KTT.rearrange("p (h t) -> p h t", t=TT))],
                       ("st", "ktt"), reads=[bKTT], writes=[bKTs[(e, ti)]])
                cx.dma("sp", [(Vs[e, h, :, ti * 4:(ti + 1) * 4, :],
                               VBF.rearrange("p (tb c) -> p tb c", c=1024)[:, :, h * 128:(h + 1) * 128]) for h in range(HA)],
                       ("st", "vbf"), reads=[bVBF], writes=[bVs[(e, ti)]])
            else:
                cx.dma("sp", [(LFC[:, s, :, :], cf[e, s].rearrange("(k p) h -> p k h", p=128)) for s in range(2)],
                       ("ld", "lfc"), writes=[bLFC])
            cx.stage("ktr%d" % l)
            logf_and_L(tile, e, fzb)
            cx.stage("logf%d" % l)
            for cb in range(2):
                Wt, bw = wget()
                for c4 in range(4):
                    hh = cb * 4 + c4
                    b = ring("main", 4)
                    mm(bPB[b], PB[b][:, 0:n], [(Wt[:, kc, c4 * 128:(c4 + 1) * 128], H[:, kc, 0:n]) for kc in range(16)], reads=[bw] + bH)
                    cx.op("act", lambda E, b=b, hh=hh: E.copy(out=QT[:, hh, 0:n], in_=PB[b][:, 0:n]), reads=[bPB[b]], writes=[bQT[hh]])
            cx.stage("q%d" % l)
            dwo = P.off[("dww", e)][0]
            pend = {}
            def att_head(h):
                pg = 32 * (h % 3)
                gq = h // 3
                if kind == "prompt":
                    ti = tile["i"]
                    nkb = 4 * (ti + 1)
                    for kt in range(ti + 1):
                        si = ring("kth", 4)
                        cx.dma("sp", [(KTH[si], KTs[e, h, :, kt * TT:(kt + 1) * TT])], ("ld", "kth", si),
                               reads=[bKTs[(e, kt)]], writes=[bKTH[si]])
                        cx.dma("sp", [(VH[si], Vs[e, h, :, kt * 4:(kt + 1) * 4, :])], ("ld", "vh", si),
                               reads=[bVs[(e, kt)]], writes=[bVH[si]])
                        for kb in range(4):
                            g = kt * 4 + kb
                            diag = (kt == ti)
                            qlo = kb * 128 if diag else 0
                            sb_ = [0, 1, 2, 3, 6][ring("st", 5)]
                            prs = [(KTH[si][:, kb * 128:(kb + 1) * 128], QT[:, h, qlo:TT]),
                                   (ONEb[pg:pg + 1, :], LQ[pg:pg + 1, gq, qlo:TT])]
                            mm(bPB[sb_], PB[sb_][:, qlo:TT], prs, reads=[bKTH[si], bQT[h], bLQ], start=True, stop=not diag)
                            if diag:
                                mm(bPB[sb_], PB[sb_][:, qlo:qlo + 128], [(IDb[:], NMb[:])], reads=[], start=False, stop=True)
                            pi = ring("pt", 3)
                            cx.op("act", lambda E, sb_=sb_, pi=pi, qlo=qlo, g=g, h=h: E.activation(
                                out=PT[pi][:, qlo:TT], in_=PB[sb_][:, qlo:TT], func=AF.Exp, bias=NEGL[:, e, g, h:h + 1], scale=float(SCALE)),
                                reads=[bPB[sb_], bNEGL[e]], writes=[bPT[pi]])
                            mm(bPB[4], PB[4][:, qlo:TT], [(VH[si][:, kb, :], PT[pi][:, qlo:TT])], reads=[bVH[si], bPT[pi]],
                               start=(g == 0), stop=(g == nkb - 1))
                            mm(bPB[5], PB[5][:, qlo:TT], [(ONEb[:], PT[pi][:, qlo:TT])], reads=[bPT[pi]],
                               start=(g == 0), stop=(g == nkb - 1))
                    i = ring("tmpf", 3)
                    cx.op("dve", lambda E, i=i: E.reciprocal(out=TMPF[i][:, :], in_=PB[5][:, :]), reads=[bPB[5]], writes=[bTMPF[i]])
                    cx.op("dve", lambda E, i=i, h=h: E.tensor_tensor(out=MOt[:, h, :], in0=PB[4][:, :], in1=TMPF[i][:, :], op=ALU.mult),
                          reads=[bPB[4], bTMPF[i]], writes=[bMO[h]])
                else:
                    for s in range(2):
                        ci = (h * 2 + s) % 2
                        cx.dma("pool", [(CK[ci], ck[e, s, :, h * 128:(h + 1) * 128].rearrange("(k p) d -> p k d", p=128))],
                               ("ld", "ck", ci), writes=[bCK[ci]])
                        cx.dma("pool", [(CVv[ci], cv[e, s, :, h * 128:(h + 1) * 128].rearrange("(k p) d -> p k d", p=128))],
                               ("ld", "cv", ci), writes=[bCV[ci]])

                        if h == 0 and s == 0:
                            cx.stage("sa%d" % l)

                        def fn(E, ci=ci):
                            ins = None
                            for k in range(8):
                                ins = E.transpose(PTP[:, k * 128:(k + 1) * 128], CK[ci][:, k, :], IDb[:])
                            return ins
                        cx.op("pe", fn, reads=[bCK[ci]], writes=[bPTP])
                        cx.op("act", lambda E, ci=ci: E.copy(out=KTC[ci], in_=PTP[:, :]), reads=[bPTP], writes=[bKTC[ci]])
                        if h == 0 and s == 0:
                            cx.stage("sb%d" % l)
                        sb_ = [0, 1, 2, 3, 6][ring("st", 5)]
                        q0 = s * 16
                        for k in range(8):
                            mm(bPB[sb_], PB[sb_][:, k * 16:(k + 1) * 16],
                               [(KTC[ci][:, k * 128:(k + 1) * 128], QT[:, h, q0:q0 + 16]),
                                (ONEb[0:1, :], LQS[0:1, h * 32 + q0:h * 32 + q0 + 16])], reads=[bKTC[ci], bQT[h], bLQ])
                        mm(bPB[sb_], PB[sb_][0:16, 128:144],
                           [(KTT.rearrange("p (h t) -> p h t", t=TT)[:, h, q0:q0 + 16], QT[:, h, q0:q0 + 16]),
                            (ONEb[0:1, 0:16], LQS[0:1, h * 32 + q0:h * 32 + q0 + 16]),
                            (IDb[0:16, 0:16], NMb[0:16, 0:16])], reads=[bKTT, bQT[h], bLQ])
                        if h == 0 and s == 0:
                            cx.stage("sc%d" % l)
                        pi = ring("pt", 3)
                        for k in range(8):
                            cx.op("act", lambda E, sb_=sb_, pi=pi, k=k, s=s, h=h: E.activation(
                                out=PT[pi][:, k * 16:(k + 1) * 16], in_=PB[sb_][:, k * 16:(k + 1) * 16], func=AF.Exp,
                                bias=NEGLS[:, s, k, h:h + 1], scale=float(SCALE)), reads=[bPB[sb_], bNEGLS], writes=[bPT[pi]])
                        cx.op("act", lambda E, sb_=sb_, pi=pi, s=s, h=h: E.activation(
                            out=PT[pi][0:16, 128:144], in_=PB[sb_][0:16, 128:144], func=AF.Exp,
                            bias=NEGLS[0:16, s, 8, h:h + 1], scale=float(SCALE)), reads=[bPB[sb_], bNEGLS], writes=[bPT[pi]])
                        if h == 0 and s == 0:
                            cx.stage("sd%d" % l)
                        prs = [(CVv[ci][:, k, :], PT[pi][:, k * 16:(k + 1) * 16]) for k in range(8)]
                        prs.append((VBF[0:16, s * 1024 + h * 128:s * 1024 + (h + 1) * 128], PT[pi][0:16, 128:144]))
                        mm(bPB[4], PB[4][:, q0:q0 + 16], prs, reads=[bCV[ci], bPT[pi], bVBF])
                        prs = [(ONEb[:], PT[pi][:, k * 16:(k + 1) * 16]) for k in range(8)]
                        prs.append((ONEb[0:16, :], PT[pi][0:16, 128:144]))
                        mm(bPB[5], PB[5][:, q0:q0 + 16], prs, reads=[bPT[pi]])
                        if h == 0 and s == 0:
                            cx.stage("se%d" % l)
                        if h == 0 and s == 1:
                            cx.stage("sf%d" % l)
                    i = ring("tmpf", 3)
                    cx.op("dve", lambda E, i=i: E.reciprocal(out=TMPF[i][:, 0:32], in_=PB[5][:, 0:32]), reads=[bPB[5]], writes=[bTMPF[i]])
                    cx.op("dve", lambda E, i=i, h=h: E.tensor_tensor(out=MOt[:, h, 0:32], in0=PB[4][:, 0:32], in1=TMPF[i][:, 0:32], op=ALU.mult),
                          reads=[bPB[4], bTMPF[i]], writes=[bMO[h]])
                    cx.stage("sh%d_%d" % (h, l))

            def glu_front(c, cc, Wt, bw):
                ba = ring("main", 4)
                mm(bPB[ba], PB[ba][:, 0:n], [(Wt[:, kc, cc * 128:(cc + 1) * 128], H[:, kc, 0:n]) for kc in range(16)], reads=[bw] + bH)
                bb = ring("main", 4)
                mm(bPB[bb], PB[bb][:, 0:n], [(Wt[:, kc, 256 + cc * 128:256 + (cc + 1) * 128], H[:, kc, 0:n]) for kc in range(16)], reads=[bw] + bH)
                i = ring("tmpf", 3)
                cx.op("act", lambda E, bb=bb, i=i: E.activation(out=TMPF[i][:, 0:n], in_=PB[bb][:, 0:n], func=AF.Sigmoid),
                      reads=[bPB[bb]], writes=[bTMPF[i]])
                if kind == "prompt":
                    ui = ring("u", 3)
                    U = Uv(ui)
                    cx.op("dve", lambda E, U=U, c=c: E.tensor_copy(out=U[:, 0:30], in_=HALOB[:, e, c, :]), reads=[bHALOB[e][c]], writes=[bU[ui]])
                    cx.op("dve", lambda E, U=U, ba=ba, i=i: E.tensor_tensor(out=U[:, 30:30 + TT], in0=PB[ba][:, :], in1=TMPF[i][:, :], op=ALU.mult),
                          reads=[bPB[ba], bTMPF[i]], writes=[bU[ui]])
                    cx.op("dve", lambda E, U=U, c=c: E.tensor_copy(out=HALOB[:, e, c, :], in_=U[:, TT:TT + 30]), reads=[bU[ui]], writes=[bHALOB[e][c]])
                    segs = [(U, 0, TT, 0)]
                    ubufs = [bU[ui]]
                else:
                    segs = []
                    ubufs = []
                    hbo = P.off[("histb", e)][0]
                    for s in range(2):
                        ui = ring("u", 3)
                        U = Uv(ui)
                        ho = hbo + (s * 8 + c) * 30
                        cx.op("dve", lambda E, U=U, ho=ho: E.tensor_copy(out=U[:, 0:30], in_=PRM[:, ho:ho + 30]), writes=[bU[ui]])
                        cx.op("dve", lambda E, U=U, ba=ba, i=i, s=s: E.tensor_tensor(
                            out=U[:, 30:46], in0=PB[ba][:, s * 16:(s + 1) * 16], in1=TMPF[i][:, s * 16:(s + 1) * 16], op=ALU.mult),
                            reads=[bPB[ba], bTMPF[i]], writes=[bU[ui]])
                        cx.op("dve", lambda E, U=U, c=c, s=s: E.tensor_copy(out=HALOB[:, e, c, :], in_=U[:, 16:46]) if s == 0 else
                              E.tensor_copy(out=MEANB[:, c * 30:(c + 1) * 30], in_=U[:, 16:46]),
                              reads=[bU[ui]], writes=[bHALOB[e][c] if s == 0 else bMEANB])
                        segs.append((U, s * 16, 16, s))
                        ubufs.append(bU[ui])
                pend[c] = (segs, ubufs)

            def conv_chain(c):
                segs, ubufs = pend[c]
                for (U, c0, w, s), bu_ in zip(segs, ubufs):
                    cx.op("dve", lambda E, U=U, c0=c0, w=w, c=c: E.tensor_scalar(
                        out=UCv(c)[:, c0:c0 + w], in0=U[:, 0:w], scalar1=PRM[:, dwo + c * 31:dwo + c * 31 + 1],
                        scalar2=pp(("dwb", e), c), op0=ALU.mult, op1=ALU.add), reads=[bu_], writes=[bUC[c]])
                    for jt in range(1, KB):
                        cx.op("dve", lambda E, U=U, c0=c0, w=w, c=c, jt=jt: E.scalar_tensor_tensor(
                            out=UCv(c)[:, c0:c0 + w], in0=U[:, jt:jt + w], scalar=PRM[:, dwo + c * 31 + jt:dwo + c * 31 + jt + 1],
                            in1=UCv(c)[:, c0:c0 + w], op0=ALU.mult, op1=ALU.add), reads=[bu_, bUC[c]], writes=[bUC[c]])

            def ln_stats(c):
                i2 = ring("sq", 2)
                cx.op("act", lambda E, c=c, i2=i2: E.activation(out=SQ[i2][:, 0:n], in_=UCv(c)[:, 0:n], func=AF.Square),
                      reads=[bUC[c]], writes=[bSQ[i2]])
                i3 = ring("ucb", 2)
                cx.op("act", lambda E, c=c, i3=i3: E.copy(out=UCB[i3][:, 0:n], in_=UCv(c)[:, 0:n]), reads=[bUC[c]], writes=[bUCB[i3]])
                mm(bPB[4], PB[4][:, 0:n], [(ONEm[:], UCB[i3][:, 0:n])], reads=[bUCB[i3]], start=(c == 0), stop=(c == 7))
                mm(bPB[5], PB[5][:, 0:n], [(ONEm[:], SQ[i2][:, 0:n])], reads=[bSQ[i2]], start=(c == 0), stop=(c == 7))

            if kind == "prompt":
                cx.alias([bKBF, bVBF, bKTT], bU + bUC)
                for h in range(HA):
                    if h % 2 == 0:
                        Wt, bw = wget()
                        for cc in range(2):
                            glu_front(h + cc, cc, Wt, bw)
                    att_head(h)
                    conv_chain(h)
                cx.stage("att%d" % l)
                for c in range(8):
                    ln_stats(c)
            else:
                for h in range(HA):
                    att_head(h)
                cx.stage("att%d" % l)
                cx.alias([bKBF, bVBF, bKTT], bU + bUC)
                for j in range(4):
                    Wt, bw = wget()
                    for cc in range(2):
                        c = 2 * j + cc
                        glu_front(c, cc, Wt, bw)
                        conv_chain(c)
                        ln_stats(c)
            cx.stage("gd%d" % l)
            cx.op("dve", lambda E: E.tensor_copy(out=MEANB[:, 0:n], in_=PB[4][:, 0:n]), reads=[bPB[4], bMEANB], writes=[bMEANB]) if kind == "prompt" else \
                cx.op("dve", lambda E: E.tensor_copy(out=TMPF[0][:, 0:n], in_=PB[4][:, 0:n]), reads=[bPB[4]], writes=[bTMPF[0]])
            MB = MEANB if kind == "prompt" else TMPF[0]
            bMB = bMEANB if kind == "prompt" else bTMPF[0]
            cx.op("dve", lambda E: E.tensor_tensor(out=TMPF[1][:, 0:n], in0=MB[:, 0:n], in1=MB[:, 0:n], op=ALU.mult), reads=[bMB], writes=[bTMPF[1]])
            cx.op("dve", lambda E: E.tensor_tensor(out=TMPF[1][:, 0:n], in0=PB[5][:, 0:n], in1=TMPF[1][:, 0:n], op=ALU.subtract),
                  reads=[bPB[5], bTMPF[1]], writes=[bTMPF[1]])
            cx.op("act", lambda E: E.activation(out=RSTD[:, 0:n], in_=TMPF[1][:, 0:n], func=AF.Sqrt, bias=EPS_L, scale=1.0),
                  reads=[bTMPF[1]], writes=[bRSTD])
            cx.op("dve", lambda E: E.reciprocal(out=RSTD[:, 0:n], in_=RSTD[:, 0:n]), reads=[bRSTD], writes=[bRSTD])
            rr["tmpf"] = 2
            cx.stage("ge%d" % l)
            for c in range(8):
                cx.op("dve", lambda E, c=c: E.tensor_tensor(out=UCv(c)[:, 0:n], in0=UCv(c)[:, 0:n], in1=MB[:, 0:n], op=ALU.subtract),
                      reads=[bUC[c], bMB], writes=[bUC[c]])
                cx.op("dve", lambda E, c=c: E.scalar_tensor_tensor(out=UCv(c)[:, 0:n], in0=UCv(c)[:, 0:n], scalar=pp(("lng", e), c),
                                                                   in1=RSTD[:, 0:n], op0=ALU.mult, op1=ALU.mult),
                      reads=[bUC[c], bRSTD], writes=[bUC[c]])
                cx.op("act", lambda E, c=c: E.activation(out=MOt[:, 8 + c, 0:n], in_=UCv(c)[:, 0:n], func=AF.Silu,
                                                         bias=pp(("lnb", e), c), scale=1.0), reads=[bUC[c]], writes=[bMO[8 + c]])
            cx.stage("glu%d" % l)
            out_proj(tile, l)
            cx.stage("outp%d" % l)
            if (kind == "prompt" and tile["i"] == NT - 1) or kind == "sample":
                nseq = 1 if kind == "prompt" else 2
                for s in range(nseq):
                    b = ring("main", 4)

                    def fn(E, b=b, s=s):
                        ins = None
                        for c in range(8):
                            src = HALOB[:, e, c, :] if s == 0 else MEANB[:, c * 30:(c + 1) * 30]
                            ins = E.transpose(PB[b][0:30, c * 128:(c + 1) * 128], src, IDf)
                        return ins
                    b2 = ring("main", 4)

                    def fn2(E, b=b, b2=b2, s=s):
                        ins = None
                        for c in range(8):
                            src = HALOB[:, e, c, :] if s == 0 else MEANB[:, c * 30:(c + 1) * 30]
                            bk = b if c < 4 else b2
                            ins = E.transpose(PB[bk][0:30, (c % 4) * 128:(c % 4 + 1) * 128], src, IDf)
                        return ins
                    cx.op("pe", fn2, reads=[bb_ for bb_ in bHALOB[e]] + [bMEANB], writes=[bPB[b], bPB[b2]])
                    i = ring("kvs", 2)
                    i2 = ring("kvs", 2)
                    cx.op("act", lambda E, b=b, i=i: E.copy(out=KVS[i][0:30, :], in_=PB[b][0:30, :]), reads=[bPB[b]], writes=[bKVS[i]])
                    cx.op("act", lambda E, b2=b2, i2=i2: E.copy(out=KVS[i2][0:30, :], in_=PB[b2][0:30, :]), reads=[bPB[b2]], writes=[bKVS[i2]])
                    oo = cbp_o[e] if kind == "prompt" else cbs_o[e, s]
                    cx.dma("sp", [(oo[:, 0:512], KVS[i][0:30, :])], ("st", "kvs", i), reads=[bKVS[i]], is_out=True)
                    cx.dma("sp", [(oo[:, 512:1024], KVS[i2][0:30, :])], ("st", "kvs", i2), reads=[bKVS[i2]], is_out=True)
            cx.alias(bU + bUC, bYS)

        def mixer_c(tile, l):
            e = l // 2
            n = tile["n"]
            kind = tile["kind"]
            norm_to_h(tile, lambda c, s: aav(l, 0, c, s), lambda c, s: modv(l, 0, c, s))
            cx.alias([bKBF, bVBF, bKTT] + bYS, bU + bUC)
            cwo = P.off[("ccw", e)][0]
            for j in range(16):
                Wt, bw = wget()
                bks = []
                for part in range(3):
                    b = ring("main", 4)
                    mm(bPB[b], PB[b][:, 0:n], [(Wt[:, kc, part * 128:(part + 1) * 128], H[:, kc, 0:n]) for kc in range(16)], reads=[bw] + bH)
                    bks.append(b)
                bbg, bcg, bxv = bks
                i = ring("tmpf", 3)
                cx.op("act", lambda E, bcg=bcg, i=i: E.copy(out=TMPF[i][:, 0:n], in_=PB[bcg][:, 0:n]), reads=[bPB[bcg]], writes=[bTMPF[i]])
                if kind == "prompt":
                    ui = ring("u", 3)
                    U = Uv(ui)
                    cx.op("dve", lambda E, U=U, j=j: E.tensor_copy(out=U[:, 0:2], in_=HALOC[:, e, j, :]), reads=[bHALOC[e][j]], writes=[bU[ui]])
                    cx.op("dve", lambda E, U=U, bxv=bxv, i=i: E.tensor_tensor(out=U[:, 2:2 + TT], in0=PB[bxv][:, :], in1=TMPF[i][:, :], op=ALU.mult),
                          reads=[bPB[bxv], bTMPF[i]], writes=[bU[ui]])
                    cx.op("dve", lambda E, U=U, j=j: E.tensor_copy(out=HALOC[:, e, j, :], in_=U[:, TT:TT + 2]), reads=[bU[ui]], writes=[bHALOC[e][j]])
                    segs = [(U, 0, TT, bU[ui])]
                else:
                    segs = []
                    hco = P.off[("histc", e)][0]
                    for s in range(2):
                        ui = ring("u", 3)
                        U = Uv(ui)
                        ho = hco + (s * 16 + j) * 2
                        cx.op("dve", lambda E, U=U, ho=ho: E.tensor_copy(out=U[:, 0:2], in_=PRM[:, ho:ho + 2]), writes=[bU[ui]])
                        cx.op("dve", lambda E, U=U, bxv=bxv, i=i, s=s: E.tensor_tensor(
                            out=U[:, 2:18], in0=PB[bxv][:, s * 16:(s + 1) * 16], in1=TMPF[i][:, s * 16:(s + 1) * 16], op=ALU.mult),
                            reads=[bPB[bxv], bTMPF[i]], writes=[bU[ui]])
                        cx.op("dve", lambda E, U=U, j=j, s=s: E.tensor_copy(out=HALOC[:, e, j, :], in_=U[:, 16:18]) if s == 0 else
                              E.tensor_copy(out=MEANB[:, j * 2:(j + 1) * 2], in_=U[:, 16:18]),
                              reads=[bU[ui]], writes=[bHALOC[e][j] if s == 0 else bMEANB])
                        segs.append((U, s * 16, 16, bU[ui]))
                uci = j % 8
                for (U, c0, w, bu_) in segs:
                    cx.op("dve", lambda E, U=U, c0=c0, w=w, j=j, uci=uci: E.tensor_scalar(
                        out=UCv(uci)[:, c0:c0 + w], in0=U[:, 0:w], scalar1=PRM[:, cwo + j * 3:cwo + j * 3 + 1], scalar2=None, op0=ALU.mult),
                        reads=[bu_], writes=[bUC[uci]])
                    for jt in (1, 2):
                        cx.op("dve", lambda E, U=U, c0=c0, w=w, j=j, jt=jt, uci=uci: E.scalar_tensor_tensor(
                            out=UCv(uci)[:, c0:c0 + w], in0=U[:, jt:jt + w], scalar=PRM[:, cwo + j * 3 + jt:cwo + j * 3 + jt + 1],
                            in1=UCv(uci)[:, c0:c0 + w], op0=ALU.mult, op1=ALU.add), reads=[bu_, bUC[uci]], writes=[bUC[uci]])
                cx.op("dve", lambda E, bbg=bbg, j=j, uci=uci: E.tensor_tensor(out=MOt[:, j, 0:n], in0=PB[bbg][:, 0:n], in1=UCv(uci)[:, 0:n], op=ALU.mult),
                      reads=[bPB[bbg], bUC[uci]], writes=[bMO[j]])
            out_proj(tile, l)
            if (kind == "prompt" and tile["i"] == NT - 1) or kind == "sample":
                nseq = 1 if kind == "prompt" else 2
                for s in range(nseq):
                    bs_ = [ring("main", 4) for _ in range(4)]

                    def fn(E, bs_=bs_, s=s):
                        ins = None
                        for c in range(16):
                            src = HALOC[:, e, c, :] if s == 0 else MEANB[:, c * 2:(c + 1) * 2]
                            ins = E.transpose(PB[bs_[c // 4]][0:2, (c % 4) * 128:(c % 4 + 1) * 128], src, IDf)
                        return ins
                    cx.op("pe", fn, reads=bHALOC[e] + [bMEANB], writes=[bPB[b] for b in bs_])
                    oo = ccp_o[e] if kind == "prompt" else ccs_o[e, s]
                    for q, b in enumerate(bs_):
                        i = ring("kvs", 2)
                        cx.op("act", lambda E, b=b, i=i: E.copy(out=KVS[i][0:2, :], in_=PB[b][0:2, :]), reads=[bPB[b]], writes=[bKVS[i]])
                        cx.dma("sp", [(oo[:, q * 512:(q + 1) * 512], KVS[i][0:2, :])], ("st", "kvs", i), reads=[bKVS[i]], is_out=True)
            cx.alias(bU + bUC, bYS)

        def final_out(tile):
            n = tile["n"]
            kind = tile["kind"]
            go = P.off["gfin"][0]
            cx.alias(bU + bUC + [bKBF, bVBF, bKTT], bYS)
            norm_sq_only(tile)
            for tb, (c0, m) in enumerate(tile["yblocks"]):
                yi = ring("ys", 2)
                for c in range(NCH):
                    i = ring("tmpf", 3)
                    cx.op("dve", lambda E, c=c, c0=c0, m=m, i=i: E.scalar_tensor_tensor(
                        out=TMPF[i][:, 0:m], in0=X[:, c, c0:c0 + m], scalar=PRM[:, go + c:go + c + 1], in1=RSTD[:, c0:c0 + m],
                        op0=ALU.mult, op1=ALU.mult), reads=[bX[c], bRSTD], writes=[bTMPF[i]])
                    b = ring("main", 4)
                    cx.op("pe", lambda E, i=i, m=m, b=b: E.transpose(PB[b][0:m, 0:128], TMPF[i][:, 0:m], IDf), reads=[bTMPF[i]], writes=[bPB[b]])
                    cx.op("act", lambda E, b=b, m=m, c=c, yi=yi: E.copy(out=YSv(yi)[0:m, c * 128:(c + 1) * 128], in_=PB[b][0:m, 0:128]),
                          reads=[bPB[b]], writes=[bYS[yi]])
                oo = y_o[tile["t0"] + c0:tile["t0"] + c0 + m, :] if kind == "prompt" else ys_o[c0:c0 + m, :]
                cx.dma("sp", [(oo, YSv(yi)[0:m, :])], ("st", "ys", yi), reads=[bYS[yi]], is_out=True)

        def norm_sq_only(tile):
            n = tile["n"]
            for c in range(NCH):
                i = ring("sq", 2)
                cx.op("act", lambda E, c=c, i=i: E.activation(out=SQ[i][:, 0:n], in_=X[:, c, 0:n], func=AF.Square),
                      reads=[bX[c]], writes=[bSQ[i]])
                mm(bPB[4], PB[4][:, 0:n], [(ONEn[:], SQ[i][:, 0:n])], reads=[bSQ[i]], start=(c == 0), stop=(c == NCH - 1))
            cx.op("act", lambda E: E.activation(out=RSTD[:, 0:n], in_=PB[4][:, 0:n], func=AF.Sqrt, bias=EPS_R, scale=1.0),
                  reads=[bPB[4]], writes=[bRSTD])
            cx.op("dve", lambda E: E.reciprocal(out=RSTD[:, 0:n], in_=RSTD[:, 0:n]), reads=[bRSTD], writes=[bRSTD])

        tiles = []
        for i in range(NT):
            tiles.append(dict(kind="prompt", i=i, t0=i * TT, n=TT, segs=[(0, TT, 0)],
                              tblocks=[(k * 128, 128) for k in range(4)], yblocks=[(k * 128, 128) for k in range(4)]))
        if with_sample:
            tiles.append(dict(kind="sample", n=32, segs=[(0, 16, 1), (16, 16, 2)], tblocks=[(0, 16), (16, 16)], yblocks=[(0, 32)]))
        for pi_, tile in enumerate(tiles):
            if pi_ > 0:
                cx.new_pass()
            if tile["kind"] == "prompt":
                cx.dma("sp", [(X[:, :, :], xT[:, tile["t0"]:tile["t0"] + TT].rearrange("(c p) t -> p c t", p=128))], ("ld", "x"), writes=bX)
            else:
                cx.alias(bKTH + bVH, bCK + bCV + bKTC)
                cx.dma("sp", [(X[:, :, 0:32], xsT.rearrange("(c p) t -> p c t", p=128))], ("ld", "x"), writes=bX)
            for l in range(4):
                if l % 2 == 0:
                    mixer_ab(tile, l)
                else:
                    mixer_c(tile, l)
                cx.stage("mix%d" % l)
                ffn(tile, l)
                cx.stage("ffn%d" % l)
                if dbg and ("x_l%d_t%d" % (l, pi_)) in dbg_o:
                    cx.dma("sp", [(dbg_o["x_l%d_t%d" % (l, pi_)].rearrange("(c p) t -> p c t", p=128), X[:, :, 0:tile["n"]])],
                           ("st", "dbg"), reads=bX, is_out=True)
            final_out(tile)
        assert cx.kstop or ws["use"] == len(blocks), (ws["use"], len(blocks))
        cx.dead = False
        cx._waits("sp", cx.out_events)
        cx.ops["sp"].append(lambda E: E.nop())

        with nc.Block() as block:
            @block.tensor
            def _(E):
                for f in cx.ops["pe"]:
                    f(E)

            @block.scalar
            def _(E):
                for f in cx.ops["act"]:
                    f(E)

            @block.vector
            def _(E):
                for f in cx.ops["dve"]:
                    f(E)

            @block.gpsimd
            def _(E):
                for f in cx.ops["pool"]:
                    f(E)

            @block.sync
            def _(E):
                for f in cx.ops["sp"]:
                    f(E)
        counts = {e: len(cx.ops[e]) for e in ENG}
        print("instr counts", counts, "sems", len(cx.semh))
    return nc


def host_prep(inp, core, T):
    P = param_layout()
    prm = np.zeros((128, P.n), np.float32)

    def put(name, arr):
        o, w = P.off[name]
        arr = np.asarray(arr, np.float32).reshape(128, w)
        prm[:, o:o + w] = arr

    fm = lambda v: np.asarray(v, np.float32).reshape(-1, 128).T
    for l in range(4):
        put(("adab", l), fm(inp["ada_b"][l]))
        put(("gmix", l), fm(inp["norm_mix_g"][l]))
        put(("gffn", l), fm(inp["norm_ffn_g"][l]))
    put("gfin", fm(inp["final_g"]))
    for e in range(2):
        put(("bf", e), np.broadcast_to(inp["b_f"][e][None, :], (128, 8)))
        put(("dww", e), inp["dw_b_w"][e].reshape(KB, 8, 128).transpose(2, 1, 0))
        put(("dwb", e), fm(inp["dw_b_bias"][e]))
        put(("lng", e), fm(inp["ln_b_g"][e]))
        put(("lnb", e), fm(inp["ln_b_b"][e]))
        put(("ccw", e), inp["conv_c_w"][e].reshape(KC, 16, 128).transpose(2, 1, 0))
        hb = inp["state_convb"][e, 2 * core:2 * core + 2]
        put(("histb", e), hb.reshape(2, 30, 8, 128).transpose(3, 0, 2, 1))
        hc = inp["state_convc"][e, 2 * core:2 * core + 2]
        put(("histc", e), hc.reshape(2, 2, 16, 128).transpose(3, 0, 2, 1))
    cs = np.stack([inp["c_prompt"][core], inp["c_sample"][2 * core], inp["c_sample"][2 * core + 1]], 0)
    put("cT", cs.reshape(3, 16, 128).transpose(2, 1, 0))
    m = {
        "xT": np.ascontiguousarray(inp["x_prompt"][core, :T].T),
        "xsT": np.ascontiguousarray(inp["x_sample"][2 * core:2 * core + 2].reshape(32, D).T),
        "ck": np.ascontiguousarray(inp["cache_k"][:, 2 * core:2 * core + 2].reshape(2, 2, PAST, DA)),
        "cv": np.ascontiguousarray(inp["cache_v"][:, 2 * core:2 * core + 2].reshape(2, 2, PAST, DA)),
        "cf": np.ascontiguousarray(inp["cache_logf"][:, 2 * core:2 * core + 2]),
        "prm": prm,
        "cst": make_consts(),
    }
    for k in ("ada_w", "w_in_ab", "w_out_ab", "w_in_c", "w_out_c", "w_gate", "w_up", "w_down"):
        m[k] = np.asarray(inp[k], np.float32)
    return m


def assemble(results, T, ncores):
    B = ncores
    f = np.float32
    y_prompt = np.stack([results[c]["y"] for c in range(B)], 0)
    y_sample = np.concatenate([results[c]["ys"].reshape(2, SS, D) for c in range(B)], 0)
    k_p = np.stack([results[c]["k_p"].reshape(2, T, HA, DH) for c in range(B)], 1)
    v_p = np.stack([results[c]["v_p"].reshape(2, T, HA, DH) for c in range(B)], 1)
    logf_p = np.stack([results[c]["logf_p"] for c in range(B)], 1)
    convb_p = np.stack([results[c]["convb_p"] for c in range(B)], 1)
    convc_p = np.stack([results[c]["convc_p"] for c in range(B)], 1)
    k_s = np.concatenate([results[c]["k_s"].reshape(2, 2, SS, HA, DH) for c in range(B)], 1)
    v_s = np.concatenate([results[c]["v_s"].reshape(2, 2, SS, HA, DH) for c in range(B)], 1)
    logf_s = np.concatenate([results[c]["logf_s"] for c in range(B)], 1)
    convb_s = np.concatenate([results[c]["convb_s"] for c in range(B)], 1)
    convc_s = np.concatenate([results[c]["convc_s"] for c in range(B)], 1)
    outs = (y_prompt, y_sample, k_p, v_p, logf_p, convb_p, convc_p, k_s, v_s, logf_s, convb_s, convc_s)
    return tuple(np.ascontiguousarray(o.astype(f)) for o in outs)


def kernel(**inputs):
    inp = {k: np.asarray(v) for k, v in inputs.items()}
    T = inp["x_prompt"].shape[1]
    ncores = 8
    nc = build(T, with_sample=True)
    in_maps = [host_prep(inp, c, T) for c in range(ncores)]
    res = run_bass_kernel_spmd(nc, in_maps, core_ids=list(range(ncores)))
    return assemble(res.results, T, ncores)
```
